# Optimizing a Trainium2 kernel written in Bass

```python
import jax
import jax.numpy as jnp
from jax import lax
import numpy as np

D_MODEL = 1024
BATCH = 8
SEQ = 2048
DEPTH = 4
DEC_BATCH = 128
DEC_SEQ = 1
PAST_LEN = 2048
PAGE_SIZE = 128

N_MIXERS = 3
N_A = (DEPTH + 2) // 3
N_B = (DEPTH + 1) // 3
N_C = DEPTH // 3

NSA_HEADS = 16
NSA_KV_HEADS = 4
NSA_GROUP = NSA_HEADS // NSA_KV_HEADS
HEAD_DIM = D_MODEL // NSA_HEADS
Q_WIDTH = NSA_HEADS * HEAD_DIM
KV_WIDTH = NSA_KV_HEADS * HEAD_DIM
NSA_IN = Q_WIDTH + 6 * KV_WIDTH + 3 * NSA_HEADS
CMP_LEN = 32
CMP_STRIDE = 16
SLC_BLOCK = 64
TOP_N = 16
WINDOW = 512
Q_BLOCK = 128
FORCE_BONUS = 1e3
ATTN_SCALE = HEAD_DIM ** -0.5
NEG_INF = -1e30

GMLP_WIDTH = D_MODEL
GMLP_GROUPS = 8
GMLP_GROUP_DIM = GMLP_WIDTH // GMLP_GROUPS
GMLP_CHUNK = 128

HGRN_EXPAND = 128
HGRN_HEADS = D_MODEL // HGRN_EXPAND
HGRN_DK = HGRN_EXPAND
HGRN_DV = D_MODEL // HGRN_HEADS
HGRN_WIDTH = HGRN_HEADS * HGRN_DK
HGRN_CHUNK = 64

FFN_HIDDEN = ((8 * D_MODEL + 3 * 256 - 1) // (3 * 256)) * 256
DEEPNORM_ALPHA = (2 * DEPTH) ** 0.25
DEEPNORM_BETA = (8 * DEPTH) ** -0.25
LN_EPS = 1e-5

kernel_name = 'nsa_gmlp_hgrn2_hybrid_step'


def layer_norm(x, g, b):
    xf = x.astype(jnp.float32)
    mu = jnp.mean(xf, axis=-1, keepdims=True)
    var = jnp.mean(jnp.square(xf - mu), axis=-1, keepdims=True)
    return ((xf - mu) * lax.rsqrt(var + LN_EPS) * g.astype(jnp.float32) + b.astype(jnp.float32)).astype(x.dtype)


def rms_norm(x, g):
    xf = x.astype(jnp.float32)
    return xf * lax.rsqrt(jnp.mean(xf * xf, axis=-1, keepdims=True) + LN_EPS) * g.astype(jnp.float32)


def deepnorm(x, h, g, b):
    return layer_norm(DEEPNORM_ALPHA * x + h, g, b)


def swiglu_ffn(x, w_in, w_out):
    gate, up = jnp.split(x @ w_in, 2, axis=-1)
    return (jax.nn.silu(gate) * up) @ w_out


def masked_attend(q, k, v, mask):
    s = jnp.einsum('bqgrd,bkgd->bqgrk', q, k).astype(jnp.float32) * ATTN_SCALE
    p = jnp.where(mask, jax.nn.softmax(jnp.where(mask, s, NEG_INF), axis=-1), 0.0)
    return jnp.einsum('bqgrk,bkgd->bqgrd', p.astype(v.dtype), v), p


def compress(rows, w, pe):
    B, L, G, Dh = rows.shape
    lhs = rows.transpose(0, 2, 1, 3).reshape(B * G, L, Dh)
    out = lax.conv_general_dilated(lhs, w, (CMP_STRIDE,), 'VALID', dimension_numbers=('NWC', 'WIO', 'NWC'))
    out = out + jnp.einsum('pd,pde->e', pe, w)
    return out.reshape(B, G, out.shape[1], Dh).transpose(0, 2, 1, 3)


def to_blocks(rows):
    B, L, G, Dh = rows.shape
    ns = -(-L // SLC_BLOCK)
    rows = jnp.pad(rows, ((0, 0), (0, ns * SLC_BLOCK - L), (0, 0), (0, 0)))
    return rows.reshape(B, ns, SLC_BLOCK, G, Dh).transpose(0, 3, 1, 2, 4)


def cmp_to_slc(n_cmp, n_slc):
    c0 = jnp.arange(n_cmp)[:, None] * CMP_STRIDE
    s0 = jnp.arange(n_slc)[None, :] * SLC_BLOCK
    ov = jnp.minimum(c0 + CMP_LEN, s0 + SLC_BLOCK) - jnp.maximum(c0, s0)
    return jnp.clip(ov, 0, None).astype(jnp.float32) / CMP_STRIDE


def nsa_project(x, w_in, b_gate):
    B, T, _ = x.shape
    q, kv, g = jnp.split(x @ w_in, [Q_WIDTH, Q_WIDTH + 6 * KV_WIDTH], axis=-1)
    q = q.reshape(B, T, NSA_KV_HEADS, NSA_GROUP, HEAD_DIM)
    kv = kv.reshape(B, T, 3, 2, NSA_KV_HEADS, HEAD_DIM)
    gates = jax.nn.sigmoid(g + b_gate).reshape(B, T, 3, NSA_KV_HEADS, NSA_GROUP)
    return q, kv, gates


def nsa_summaries(rows_cmp, rows_slc, w_cmp, pe_cmp):
    kc = compress(rows_cmp[:, :, 0], w_cmp[0], pe_cmp[0])
    vc = compress(rows_cmp[:, :, 1], w_cmp[1], pe_cmp[1])
    n_cmp = kc.shape[1]
    c_end = jnp.arange(n_cmp) * CMP_STRIDE + CMP_LEN - 1
    ks_blk = to_blocks(rows_slc[:, :, 0])
    vs_blk = to_blocks(rows_slc[:, :, 1])
    return kc, vc, c_end, ks_blk, vs_blk, cmp_to_slc(n_cmp, ks_blk.shape[2])


def nsa_branches(q, gates, q_pos, kc, vc, c_end, ks_blk, vs_blk, kw, vw, w_pos, cmap):
    B, Tq, G, R, Dh = q.shape
    qp = q_pos[:, None]
    o_cmp, p_cmp = masked_attend(q, kc, vc, (c_end[None, :] <= qp)[None, :, None, None, :])
    n_slc = ks_blk.shape[2]
    blk = jnp.arange(n_slc)
    cur = (q_pos // SLC_BLOCK)[None, :, None, None]
    forced = (blk == 0) | (blk == cur) | (blk == cur - 1)
    importance = jnp.einsum('bqgrc,cs->bqgs', p_cmp, cmap)
    score = jnp.where(blk <= cur, importance + FORCE_BONUS * forced, -jnp.inf)
    n_top = min(TOP_N, n_slc)
    top_score, top_idx = lax.top_k(score, n_top)
    b_ix = jnp.arange(B)[:, None, None, None]
    g_ix = jnp.arange(G)[None, None, :, None]
    k_sel = ks_blk[b_ix, g_ix, top_idx]
    v_sel = vs_blk[b_ix, g_ix, top_idx]
    tok_pos = top_idx[..., None] * SLC_BLOCK + jnp.arange(SLC_BLOCK)
    sel_mask = jnp.isfinite(top_score)[..., None] & (tok_pos <= q_pos[None, :, None, None, None])
    sel_mask = sel_mask.reshape(B, Tq, G, 1, n_top * SLC_BLOCK)
    s = jnp.einsum('bqgrd,bqgnjd->bqgrnj', q, k_sel).astype(jnp.float32) * ATTN_SCALE
    p = jax.nn.softmax(jnp.where(sel_mask, s.reshape(B, Tq, G, R, -1), NEG_INF), axis=-1)
    o_slc = jnp.einsum('bqgrnj,bqgnjd->bqgrd', p.reshape(s.shape).astype(v_sel.dtype), v_sel)
    wp = w_pos[None, :]
    win_mask = (wp <= qp) & (wp > qp - WINDOW) & (wp >= 0)
    o_win, _ = masked_attend(q, kw, vw, win_mask[None, :, None, None, :])
    return (gates[:, :, 0, ..., None] * o_cmp + gates[:, :, 1, ..., None] * o_slc
            + gates[:, :, 2, ..., None] * o_win)


def nsa_prompt(x, w_in, b_gate, w_cmp, pe_cmp, w_out):
    B, T, _ = x.shape
    q, kv, gates = nsa_project(x, w_in, b_gate)
    kc, vc, c_end, ks_blk, vs_blk, cmap = nsa_summaries(kv[:, :, 0], kv[:, :, 1], w_cmp, pe_cmp)
    win_pad = jnp.pad(kv[:, :, 2], ((0, 0), (WINDOW, 0), (0, 0), (0, 0), (0, 0)))
    qb = Q_BLOCK if T % Q_BLOCK == 0 else T
    nb = T // qb

    def block(n):
        b, s0 = n // nb, (n % nb) * qb
        pick = lambda a: lax.dynamic_index_in_dim(a, b, 0, keepdims=True)
        rows = lambda a, size: lax.dynamic_slice_in_dim(pick(a), s0, size, axis=1)
        w_rows = rows(win_pad, qb + WINDOW)
        o = nsa_branches(rows(q, qb), rows(gates, qb), s0 + jnp.arange(qb), pick(kc), pick(vc), c_end,
                         pick(ks_blk), pick(vs_blk), w_rows[:, :, 0], w_rows[:, :, 1],
                         s0 - WINDOW + jnp.arange(qb + WINDOW), cmap)
        return o[0]

    o = lax.map(block, jnp.arange(B * nb))
    return o.reshape(B, T, Q_WIDTH) @ w_out, kv


def nsa_sample(x, cache_cmp, cache_slc, cache_win, page_table, w_in, b_gate, w_cmp, pe_cmp, w_out):
    B, T, _ = x.shape
    past = page_table.shape[1] * PAGE_SIZE
    wb = cache_win.shape[1]
    q, kv, gates = nsa_project(x, w_in, b_gate)
    paged = lambda cache: cache[page_table].reshape((B, past) + cache.shape[2:])
    rows_cmp = jnp.concatenate([paged(cache_cmp), kv[:, :, 0]], axis=1)
    rows_slc = jnp.concatenate([paged(cache_slc), kv[:, :, 1]], axis=1)
    rows_win = jnp.concatenate([cache_win, kv[:, :, 2]], axis=1)
    kc, vc, c_end, ks_blk, vs_blk, cmap = nsa_summaries(rows_cmp, rows_slc, w_cmp, pe_cmp)
    q_pos = past + jnp.arange(T)
    w_pos = jnp.concatenate([past - wb + jnp.arange(wb), q_pos])
    o = nsa_branches(q, gates, q_pos, kc, vc, c_end, ks_blk, vs_blk, rows_win[:, :, 0], rows_win[:, :, 1],
                     w_pos, cmap)
    return o.reshape(B, T, Q_WIDTH) @ w_out, kv


def gmlp_mixer(x, w_in, b_in, ln_v, w_sp, b_sp, w_out):
    B, T, _ = x.shape
    u, v = jnp.split(jax.nn.gelu(x @ w_in + b_in, approximate=False), 2, axis=-1)
    v = layer_norm(v, ln_v[0], ln_v[1])
    c = min(T, GMLP_CHUNK)
    nc = T // c
    w = jnp.tril(w_sp[:, :c, :c])
    vh = v.reshape(B, nc, c, GMLP_GROUPS, GMLP_GROUP_DIM)
    mixed = jnp.einsum('hts,bcshd->bcthd', w, vh) + b_sp[:, :c].T[:, :, None]
    return (u * mixed.reshape(B, T, GMLP_WIDTH)) @ w_out, v


def hgrn2_mixer(x, s0, lb, w_in, norm_gain, w_out):
    B, T, _ = x.shape
    q, f, i, g = jnp.split(x @ w_in, 4, axis=-1)
    heads = lambda a: a.reshape(B, T, HGRN_HEADS, -1).astype(jnp.float32)
    lb = lb.reshape(HGRN_HEADS, HGRN_DK)
    log_f = jnp.logaddexp(jnp.log(lb), jnp.log1p(-lb) + jax.nn.log_sigmoid(heads(f)))
    k = -jnp.expm1(log_f)
    c = HGRN_CHUNK if T % HGRN_CHUNK == 0 else T
    nc = T // c
    chunks = lambda a: a.reshape(B, nc, c, HGRN_HEADS, -1).transpose(1, 0, 3, 2, 4)
    causal = jnp.tril(jnp.ones((c, c), dtype=bool))[:, :, None]

    def step(S, inp):
        qc, kc, vc, lfc = inp
        cum = jnp.cumsum(lfc, axis=2)
        o_inter = jnp.einsum('bhtk,bhkv->bhtv', qc * jnp.exp(cum), S)
        decay = jnp.exp(jnp.where(causal, cum[:, :, :, None, :] - cum[:, :, None, :, :], -jnp.inf))
        attn = jnp.einsum('bhtk,bhsk,bhtsk->bhts', qc, kc, decay)
        o_intra = jnp.einsum('bhts,bhsv->bhtv', attn, vc)
        last = cum[:, :, -1:, :]
        S = jnp.exp(last[:, :, 0, :, None]) * S + jnp.einsum('bhsk,bhsv->bhkv', kc * jnp.exp(last - cum), vc)
        return S, o_inter + o_intra

    S, o = lax.scan(step, s0.astype(jnp.float32), (chunks(heads(q)), chunks(k), chunks(heads(i)), chunks(log_f)))
    o = o.transpose(1, 0, 3, 2, 4).reshape(B, T, HGRN_HEADS, HGRN_DV)
    o = rms_norm(o, norm_gain).reshape(B, T, HGRN_WIDTH).astype(x.dtype)
    return (o * jax.nn.silu(g)) @ w_out, S.astype(s0.dtype)


def setup_inputs(seed: int = 0) -> dict:
    key = jax.random.key(seed)
    ks = iter(jax.random.split(key, 40))
    nrm = lambda shape, scale: jax.random.normal(next(ks), shape, jnp.float32) * scale
    n_pages = PAST_LEN // PAGE_SIZE
    n_phys = (5 * DEC_BATCH * n_pages + 3) // 4
    win_buf = min(WINDOW, PAST_LEN)
    beta = DEEPNORM_BETA
    page_table = jax.random.permutation(next(ks), n_phys)[:DEC_BATCH * n_pages]
    page_table = page_table.reshape(DEC_BATCH, n_pages).astype(jnp.int32)
    return {
        'x_prompt': nrm((BATCH, SEQ, D_MODEL), 1.0),
        'x_sample': nrm((DEC_BATCH, DEC_SEQ, D_MODEL), 1.0),
        'cache_cmp_kv': nrm((N_A, n_phys, PAGE_SIZE, 2, NSA_KV_HEADS, HEAD_DIM), 1.0),
        'cache_slc_kv': nrm((N_A, n_phys, PAGE_SIZE, 2, NSA_KV_HEADS, HEAD_DIM), 1.0),
        'cache_win_kv': nrm((N_A, DEC_BATCH, win_buf, 2, NSA_KV_HEADS, HEAD_DIM), 1.0),
        'state_hgrn': nrm((N_C, DEC_BATCH, HGRN_HEADS, HGRN_DK, HGRN_DV), 0.5),
        'page_table': page_table,
        'ln_gain': 1.0 + nrm((DEPTH, 2, D_MODEL), 0.02),
        'ln_bias': nrm((DEPTH, 2, D_MODEL), 0.02),
        'ffn_w_in': nrm((DEPTH, D_MODEL, 2 * FFN_HIDDEN), D_MODEL ** -0.5),
        'ffn_w_out': nrm((DEPTH, FFN_HIDDEN, D_MODEL), beta * FFN_HIDDEN ** -0.5),
        'nsa_w_in': nrm((N_A, D_MODEL, NSA_IN), D_MODEL ** -0.5),
        'nsa_b_gate': nrm((N_A, 3 * NSA_HEADS), 0.02),
        'nsa_w_cmp': nrm((N_A, 2, CMP_LEN, HEAD_DIM, HEAD_DIM), (CMP_LEN * HEAD_DIM) ** -0.5),
        'nsa_pe_cmp': nrm((N_A, 2, CMP_LEN, HEAD_DIM), 0.1),
        'nsa_w_out': nrm((N_A, Q_WIDTH, D_MODEL), beta * Q_WIDTH ** -0.5),
        'gmlp_w_in': nrm((N_B, D_MODEL, 2 * GMLP_WIDTH), D_MODEL ** -0.5),
        'gmlp_b_in': nrm((N_B, 2 * GMLP_WIDTH), 0.02),
        'gmlp_ln_v': jnp.stack([1.0 + nrm((N_B, GMLP_WIDTH), 0.02), nrm((N_B, GMLP_WIDTH), 0.02)], axis=1),
        'gmlp_w_sp': nrm((N_B, GMLP_GROUPS, GMLP_CHUNK, GMLP_CHUNK), GMLP_CHUNK ** -0.5),
        'gmlp_b_sp': 1.0 + nrm((N_B, GMLP_GROUPS, GMLP_CHUNK), 0.02),
        'gmlp_w_out': nrm((N_B, GMLP_WIDTH, D_MODEL), beta * GMLP_WIDTH ** -0.5),
        'hgrn_w_in': nrm((N_C, D_MODEL, 4 * HGRN_WIDTH), D_MODEL ** -0.5),
        'hgrn_lb_logits': nrm((DEPTH, HGRN_WIDTH), 0.5),
        'hgrn_norm_gain': 1.0 + nrm((N_C, HGRN_DV), 0.02),
        'hgrn_w_out': nrm((N_C, HGRN_WIDTH, D_MODEL), beta * HGRN_WIDTH ** -0.5),
    }


def reference(x_prompt, x_sample, cache_cmp_kv, cache_slc_kv, cache_win_kv, state_hgrn, page_table,
              ln_gain, ln_bias, ffn_w_in, ffn_w_out,
              nsa_w_in, nsa_b_gate, nsa_w_cmp, nsa_pe_cmp, nsa_w_out,
              gmlp_w_in, gmlp_b_in, gmlp_ln_v, gmlp_w_sp, gmlp_b_sp, gmlp_w_out,
              hgrn_w_in, hgrn_lb_logits, hgrn_norm_gain, hgrn_w_out):
    lb_sm = jax.nn.softmax(hgrn_lb_logits.astype(jnp.float32), axis=0)
    lower_bounds = jnp.cumsum(lb_sm, axis=0) - lb_sm[0]
    xp, xs = x_prompt, x_sample
    B, T, _ = xp.shape
    wb_prompt = min(WINDOW, T)
    cmp_p, cmp_s, slc_p, slc_s, win_p, win_s, gv_s, hs_p, hs_s = [], [], [], [], [], [], [], [], []
    for layer in range(DEPTH):
        kind, j = layer % N_MIXERS, layer // N_MIXERS
        if kind == 0:
            hp, kvp = nsa_prompt(xp, nsa_w_in[j], nsa_b_gate[j], nsa_w_cmp[j], nsa_pe_cmp[j], nsa_w_out[j])
            hs, kvs = nsa_sample(xs, cache_cmp_kv[j], cache_slc_kv[j], cache_win_kv[j], page_table,
                                 nsa_w_in[j], nsa_b_gate[j], nsa_w_cmp[j], nsa_pe_cmp[j], nsa_w_out[j])
            page_shape = (B * T // PAGE_SIZE, PAGE_SIZE) + kvp.shape[3:]
            cmp_p.append(kvp[:, :, 0].reshape(page_shape))
            slc_p.append(kvp[:, :, 1].reshape(page_shape))
            win_p.append(kvp[:, T - wb_prompt:, 2])
            cmp_s.append(kvs[:, :, 0])
            slc_s.append(kvs[:, :, 1])
            win_s.append(kvs[:, :, 2])
        elif kind == 1:
            hp, _ = gmlp_mixer(xp, gmlp_w_in[j], gmlp_b_in[j], gmlp_ln_v[j], gmlp_w_sp[j], gmlp_b_sp[j], gmlp_w_out[j])
            hs, v_new = gmlp_mixer(xs, gmlp_w_in[j], gmlp_b_in[j], gmlp_ln_v[j], gmlp_w_sp[j], gmlp_b_sp[j], gmlp_w_out[j])
            gv_s.append(v_new)
        else:
            s_zero = jnp.zeros((B, HGRN_HEADS, HGRN_DK, HGRN_DV), xp.dtype)
            hp, sp = hgrn2_mixer(xp, s_zero, lower_bounds[layer], hgrn_w_in[j], hgrn_norm_gain[j], hgrn_w_out[j])
            hs, ss = hgrn2_mixer(xs, state_hgrn[j], lower_bounds[layer], hgrn_w_in[j], hgrn_norm_gain[j], hgrn_w_out[j])
            hs_p.append(sp)
            hs_s.append(ss)
        xp = deepnorm(xp, hp, ln_gain[layer, 0], ln_bias[layer, 0])
        xs = deepnorm(xs, hs, ln_gain[layer, 0], ln_bias[layer, 0])
        xp = deepnorm(xp, swiglu_ffn(xp, ffn_w_in[layer], ffn_w_out[layer]), ln_gain[layer, 1], ln_bias[layer, 1])
        xs = deepnorm(xs, swiglu_ffn(xs, ffn_w_in[layer], ffn_w_out[layer]), ln_gain[layer, 1], ln_bias[layer, 1])
    y_prompt, y_sample = xp, xs
    new_cmp_kv_prompt = jnp.stack(cmp_p)
    new_cmp_kv_sample = jnp.stack(cmp_s)
    new_slc_kv_prompt = jnp.stack(slc_p)
    new_slc_kv_sample = jnp.stack(slc_s)
    new_win_kv_prompt = jnp.stack(win_p)
    new_win_kv_sample = jnp.stack(win_s)
    new_gmlp_v_sample = jnp.stack(gv_s)
    new_hgrn_prompt = jnp.stack(hs_p)
    new_hgrn_sample = jnp.stack(hs_s)
    return (y_prompt, y_sample, new_cmp_kv_prompt, new_cmp_kv_sample, new_slc_kv_prompt, new_slc_kv_sample,
            new_win_kv_prompt, new_win_kv_sample, new_gmlp_v_sample, new_hgrn_prompt, new_hgrn_sample)
```

```python
import numpy as np
from contextlib import ExitStack
import concourse.bass as bass
import concourse.mybir as mybir
from concourse.bass_utils import run_bass_kernel_spmd

F32 = mybir.dt.float32
BF16 = mybir.dt.bfloat16
I32 = mybir.dt.int32
U32 = mybir.dt.uint32
ALU = mybir.AluOpType
AF = mybir.ActivationFunctionType
AX = mybir.AxisListType

ENGS = ["pe", "dve", "act", "pool", "sp"]
EPOCH = 4096
NDMASEM = 24

D = 1024
T = 2048
NS = 16
NT = T + NS
DEPTH = 4
FH = 2816
ALPHA = (2 * DEPTH) ** 0.25
EPS = 1e-5
SCALE = 64 ** -0.5
NEG = -30000.0
TILES = [(0, 512), (512, 512), (1024, 512), (1536, 512), (2048, 16)]
GROUPS = [[0, 1], [2, 3, 4]]
GSTART = [0, 1024]


class Dep:
    __slots__ = ("w", "r")

    def __init__(self):
        self.w = None
        self.r = {}


class Prog:
    def __init__(self):
        self.nc = bass.Bass("TRN2", target_bir_lowering=False)
        self.es = ExitStack()
        self.cnt = {e: 0 for e in ENGS}
        self.known = {e: {} for e in ENGS}
        self.sems = {}
        self.ndma = 0
        self.dma_last = {}
        self.uid = 0

    def sb(self, shape, dt=F32, st=None, name=None):
        self.uid += 1
        return (st or self.es).enter_context(self.nc.sbuf_tensor(f"{name or 't'}{self.uid}", list(shape), dt))

    def ps(self, shape, dt=F32):
        self.uid += 1
        return self.es.enter_context(self.nc.psum_tensor(f"ps{self.uid}", list(shape), dt))

    def dram(self, name, shape, dt, kind):
        return self.nc.dram_tensor(name, list(shape), dt, kind=kind).ap()

    def sem(self, key):
        if key not in self.sems:
            self.sems[key] = self.es.enter_context(self.nc.semaphore("s_" + "_".join(str(k) for k in key)))
        return self.sems[key]

    def _needs(self, eng, reads, writes):
        need = {}

        def add(rec):
            if rec is None:
                return
            k, v = rec
            if eng == "pe" and k[0] == "e" and k[1] == "pe":
                return
            if need.get(k, 0) < v:
                need[k] = v
        for d in reads:
            add(d.w)
        for d in writes:
            add(d.w)
            for k, v in d.r.items():
                add((k, v))
        kn = self.known[eng]
        out = []
        for k, v in need.items():
            if kn.get(k, 0) < v:
                kn[k] = v
                out.append((k, v))
        return out

    def _commit(self, rec, reads, writes):
        k, v = rec
        for d in reads:
            if d.r.get(k, 0) < v:
                d.r[k] = v
        for d in writes:
            d.w = rec
            d.r = {}

    def _eng(self, eng):
        nc = self.nc
        return {"pe": nc.tensor, "dve": nc.vector, "act": nc.scalar, "pool": nc.gpsimd, "sp": nc.sync}[eng]

    def _emit(self, eng, waits, fn, key, inc):
        e = self._eng(eng)
        for k, v in waits:
            e.wait_ge(self.sem(k), v)
        ins = fn(e)
        ins.then_inc(self.sems[key], inc)

    def op(self, eng, fn, reads=(), writes=()):
        waits = self._needs(eng, reads, writes)
        n = self.cnt[eng]
        self.cnt[eng] = n + 1
        key = ("e", eng, n // EPOCH)
        self.sem(key)
        self._emit(eng, waits, fn, key, 1)
        self._commit((key, n % EPOCH + 1), reads, writes)

    def dma(self, fn, reads=(), writes=(), eng="sp"):
        i = self.ndma % NDMASEM
        self.ndma += 1
        key = ("d", i)
        prev = self.dma_last.get(i, 0)
        waits = self._needs(eng, reads, writes)
        if prev and self.known[eng].get(key, 0) < prev:
            self.known[eng][key] = prev
            waits.append((key, prev))
        self.dma_last[i] = prev + 16
        self.sem(key)
        self._emit(eng, waits, fn, key, 16)
        self._commit((key, prev + 16), reads, writes)

    def drain(self, eng):
        n = self.cnt[eng]
        if n:
            k, v = ("e", eng, (n - 1) // EPOCH), (n - 1) % EPOCH + 1
            self._eng(eng).wait_ge(self.sem(k), v)

    def fence(self):
        recs = []
        for e2 in ENGS:
            n = self.cnt[e2]
            if n:
                recs.append((("e", e2, (n - 1) // EPOCH), (n - 1) % EPOCH + 1))
        for i, v in self.dma_last.items():
            recs.append((("d", i), v))
        for e in ENGS:
            eng = self._eng(e)
            for k, v in recs:
                if self.known[e].get(k, 0) < v:
                    self.known[e][k] = v
                    eng.wait_ge(self.sem(k), v)

    def finish(self):
        self.fence()
        self.es.close()
        return self.nc


class K:
    def __init__(self, n_phys, layers=DEPTH, do_sample_nsa=True):
        self.p = p = Prog()
        self.n_phys = n_phys
        self.layers = layers
        self.do_sample_nsa = do_sample_nsa
        nc = p.nc
        I = lambda n, s, dt=F32: p.dram(n, s, dt, "ExternalInput")
        O = lambda n, s, dt=F32: p.dram(n, s, dt, "ExternalOutput")
        self.x_prompt = I("x_prompt", [T, D])
        self.x_sample = I("x_sample", [NS, D])
        self.cache_cmp = I("cache_cmp_kv", [2 * n_phys * 128, 512])
        self.cache_slc = I("cache_slc_kv", [2 * n_phys * 128, 512])
        self.cache_win = I("cache_win_kv", [2, NS, 512, 512])
        self.state_hgrn = I("state_hgrn", [NS, 8, 128, 128])
        self.page_table = I("page_table", [NS, 16], I32)
        self.ln_gain = I("ln_gain", [DEPTH, 2, D])
        self.ln_bias = I("ln_bias", [DEPTH, 2, D])
        self.ffn_w_in = I("ffn_w_in", [DEPTH, D, 2 * FH])
        self.ffn_w_out = I("ffn_w_out", [DEPTH, FH, D])
        self.nsa_w_in = I("nsa_w_in", [2, D, 2608])
        self.nsa_b_gate = I("nsa_b_gate", [2, 48])
        self.nsa_w_cmp = I("nsa_w_cmp", [2, 2, 32, 64, 64])
        self.nsa_pe_cmp = I("nsa_pe_cmp", [2, 2, 32, 64])
        self.nsa_w_out = I("nsa_w_out", [2, D, D])
        self.gmlp_w_in = I("gmlp_w_in", [D, 2 * D])
        self.gmlp_b_in = I("gmlp_b_in", [2 * D])
        self.gmlp_ln_v = I("gmlp_ln_v", [2, D])
        self.gmlp_w_sp = I("gmlp_w_sp", [8, 128, 128])
        self.gmlp_b_sp = I("gmlp_b_sp", [8, 128])
        self.gmlp_w_out = I("gmlp_w_out", [D, D])
        self.hgrn_w_in = I("hgrn_w_in", [D, 4 * D])
        self.hgrn_lb_logits = I("hgrn_lb_logits", [DEPTH, D])
        self.hgrn_norm_gain = I("hgrn_norm_gain", [128])
        self.hgrn_w_out = I("hgrn_w_out", [D, D])
        self.y_prompt = O("y_prompt", [T, D])
        self.y_sample = O("y_sample", [NS, D])
        self.o_cmp_p = O("o_cmp_p", [2, T, 512])
        self.o_cmp_s = O("o_cmp_s", [2, NS, 512])
        self.o_slc_p = O("o_slc_p", [2, T, 512])
        self.o_slc_s = O("o_slc_s", [2, NS, 512])
        self.o_win_p = O("o_win_p", [2, 512, 512])
        self.o_win_s = O("o_win_s", [2, NS, 512])
        self.o_gv_s = O("o_gv_s", [NS, D])
        self.o_hg_p = O("o_hg_p", [8, 128, 128])
        self.o_hg_s = O("o_hg_s", [NS, 8, 128, 128])

        self.xres = p.sb([128, 8, NT], F32, name="xres")
        self.xres_d = [Dep() for _ in TILES]
        self.xb = p.sb([128, 8, 1040], BF16, name="xb")
        self.xb_d = [Dep() for _ in range(3)]
        self.wbuf = [p.sb([128, 4096], BF16, name="wbuf") for _ in range(2)]
        self.wbuf_d = [Dep() for _ in range(2)]
        self.wi = 0
        self.banks = [p.ps([128, 512], F32) for _ in range(8)]
        self.bank_d = [Dep() for _ in range(8)]
        self.bi = 0
        self.held = set()
        self.cd = Dep()
        self.ones_bf = p.sb([128, 128], BF16, name="ones_bf")
        self.ones_f = p.sb([1, 128], F32, name="ones_f")
        self.onec_f = p.sb([128, 1], F32, name="onec_f")
        self.ident_f = p.sb([128, 128], F32, name="ident_f")
        self.ident_bf = p.sb([128, 128], BF16, name="ident_bf")
        self.lng = p.sb([128, DEPTH * 2 * 8], F32, name="lng")
        self.lnb = p.sb([128, DEPTH * 2 * 8], F32, name="lnb")
        self.tmp_d = {}
        self.build()

    def bank(self, hold=False):
        while True:
            i = self.bi % 8
            self.bi += 1
            if i not in self.held:
                break
        if hold:
            self.held.add(i)
            return self.banks[i], self.bank_d[i], i
        return self.banks[i], self.bank_d[i]

    def release(self, i):
        self.held.discard(i)

    def getw(self):
        i = self.wi % 2
        self.wi += 1
        return self.wbuf[i], self.wbuf_d[i]

    def substack(self):
        import contextlib

        @contextlib.contextmanager
        def cm():
            st = ExitStack()
            try:
                yield st
            finally:
                self.p.fence()
                st.close()
        return cm()

    def rot(self, name, n, shape, dt, st):
        d = st.__dict__.setdefault("_rot", {})
        if name not in d:
            d[name] = [[(self.p.sb(shape, dt, st=st, name=name), Dep()) for _ in range(n)], 0]
        ent = d[name]
        t = ent[0][ent[1] % n]
        ent[1] += 1
        return t

    def setup_consts(self):
        p = self.p
        cd = self.cd
        p.op("dve", lambda e: e.memset(self.ones_bf[:], 1.0), writes=[cd])
        p.op("dve", lambda e: e.memset(self.ones_f[:], 1.0), writes=[cd])
        p.op("dve", lambda e: e.memset(self.onec_f[:], 1.0), writes=[cd])
        p.op("pool", lambda e: e.iota(self.ident_f[:], pattern=[[1, 128]], base=0, channel_multiplier=-1,
                                      allow_small_or_imprecise_dtypes=True), writes=[cd])
        p.op("dve", lambda e: e.tensor_single_scalar(out=self.ident_f[:], in_=self.ident_f[:], scalar=0.0,
                                                     op=ALU.is_equal), reads=[cd], writes=[cd])
        p.op("dve", lambda e: e.tensor_copy(out=self.ident_bf[:], in_=self.ident_f[:]), reads=[cd], writes=[cd])
        for l in range(DEPTH):
            for s in range(2):
                o = (l * 2 + s) * 8
                p.dma(lambda e, l=l, s=s, o=o: e.dma_start(
                    out=self.lng[:, o:o + 8], in_=self.ln_gain[l, s].rearrange("(c p) -> p c", p=128),
                    allow_slow_non_contiguous=True), writes=[cd])
                p.dma(lambda e, l=l, s=s, o=o: e.dma_start(
                    out=self.lnb[:, o:o + 8], in_=self.ln_bias[l, s].rearrange("(c p) -> p c", p=128),
                    allow_slow_non_contiguous=True), writes=[cd])

    def load_x(self):
        p = self.p
        with ExitStack() as st:
            for ti, (t0, n) in enumerate(TILES):
                for b0 in range(0, n, 128):
                    nb = min(128, n - b0)
                    xt, xd = self.rot("xin", 2, [128, D], F32, st)
                    src = self.x_prompt[t0 + b0:t0 + b0 + nb, :] if ti < 4 else self.x_sample[:, :]
                    p.dma(lambda e, xt=xt, src=src, nb=nb: e.dma_start(out=xt[:nb, :], in_=src), writes=[xd])
                    for half in range(2):
                        bk, bd = self.bank()
                        for cc in range(4):
                            c = half * 4 + cc
                            p.op("pe", lambda e, bk=bk, xt=xt, c=c, cc=cc, nb=nb: e.transpose(
                                out=bk[:, cc * 128:cc * 128 + nb], in_=xt[:nb, c * 128:(c + 1) * 128],
                                identity=self.ident_f[:nb, :nb]), reads=[xd, self.cd], writes=[bd])
                        p.op("act", lambda e, bk=bk, half=half, t0=t0, b0=b0, nb=nb: e.copy(
                            out=self.xres[:, half * 4:half * 4 + 4, t0 + b0:t0 + b0 + nb],
                            in_=bk[:, :].rearrange("p (c t) -> p c t", c=4)[:, :, :nb]),
                            reads=[bd], writes=[self.xres_d[ti]])
            p.fence()

    def store_y(self):
        p = self.p
        with ExitStack() as st:
            for ti, (t0, n) in enumerate(TILES):
                for b0 in range(0, n, 128):
                    nb = min(128, n - b0)
                    yt, yd = self.rot("yout", 2, [128, D], F32, st)
                    for half in range(2):
                        bk, bd = self.bank()
                        for cc in range(4):
                            c = half * 4 + cc
                            p.op("pe", lambda e, bk=bk, c=c, cc=cc, nb=nb, t0=t0, b0=b0: e.transpose(
                                out=bk[:nb, cc * 128:(cc + 1) * 128], in_=self.xres[:, c, t0 + b0:t0 + b0 + nb],
                                identity=self.ident_f[:, :]), reads=[self.xres_d[ti], self.cd], writes=[bd])
                        p.op("act", lambda e, bk=bk, yt=yt, half=half, nb=nb: e.copy(
                            out=yt[:nb, half * 512:(half + 1) * 512], in_=bk[:nb, :]), reads=[bd], writes=[yd])
                    dst = self.y_prompt[t0 + b0:t0 + b0 + nb, :] if ti < 4 else self.y_sample[:, :]
                    p.dma(lambda e, yt=yt, dst=dst, nb=nb: e.dma_start(out=dst, in_=yt[:nb, :]), reads=[yd])
            p.fence()

    def cast_xb(self, g):
        p = self.p
        for tl, ti in enumerate(GROUPS[g]):
            t0, n = TILES[ti]
            off = t0 - GSTART[g]
            p.op("dve", lambda e, t0=t0, n=n, off=off: e.tensor_copy(
                out=self.xb[:, :, off:off + n], in_=self.xres[:, :, t0:t0 + n]),
                reads=[self.xres_d[ti]], writes=[self.xb_d[tl]])

    def layernorm(self, ti, lnidx, st):
        p = self.p
        t0, n = TILES[ti]
        xd = self.xres_d[ti]
        zs, zsd = self.rot("zs", 1, [128, 8, 512], BF16, st)
        p.op("pool", lambda e: e.tensor_tensor(out=zs[:, :, :n], in0=self.xres[:, :, t0:t0 + n],
                                               in1=self.xres[:, :, t0:t0 + n], op=ALU.mult), reads=[xd], writes=[zsd])
        s1, s1d = self.bank()
        s2, s2d = self.bank()
        for c in range(8):
            p.op("pe", lambda e, c=c: e.matmul(s1[0:1, :n], lhsT=self.onec_f[:, 0:1], rhs=self.xres[:, c, t0:t0 + n],
                                               start=(c == 0), stop=(c == 7)), reads=[xd, self.cd], writes=[s1d])
        for c in range(8):
            p.op("pe", lambda e, c=c: e.matmul(s2[0:1, :n], lhsT=self.ones_bf[:, 0:1], rhs=zs[:, c, :n],
                                               start=(c == 0), stop=(c == 7)), reads=[zsd, self.cd], writes=[s2d])
        stt, std = self.rot("lnst", 1, [1, 3, 512], F32, st)
        mean, msq, nmr = stt[:, 0, :n], stt[:, 1, :n], stt[:, 2, :n]
        rstd = msq
        p.op("dve", lambda e: e.tensor_scalar(out=mean, in0=s1[0:1, :n], scalar1=1.0 / D, scalar2=None,
                                              op0=ALU.mult), reads=[s1d], writes=[std])
        p.op("dve", lambda e: e.tensor_tensor(out=msq, in0=mean, in1=mean, op=ALU.mult), reads=[std], writes=[std])
        p.op("dve", lambda e: e.scalar_tensor_tensor(out=msq, in0=s2[0:1, :n], scalar=1.0 / D, in1=msq,
                                                     op0=ALU.mult, op1=ALU.subtract), reads=[s2d, std], writes=[std])
        p.op("act", lambda e: e.activation(out=rstd, in_=msq, func=AF.Sqrt, bias=EPS, scale=1.0),
             reads=[std], writes=[std])
        p.op("dve", lambda e: e.reciprocal(out=rstd, in_=rstd), reads=[std], writes=[std])
        p.op("dve", lambda e: e.scalar_tensor_tensor(out=nmr, in0=mean, scalar=-1.0, in1=rstd,
                                                     op0=ALU.mult, op1=ALU.mult), reads=[std], writes=[std])
        A, Ad = self.bank()
        B, Bd = self.bank()
        p.op("pe", lambda e: e.matmul(A[:, :n], lhsT=self.ones_f[0:1, :], rhs=rstd, start=True, stop=True),
             reads=[std, self.cd], writes=[Ad])
        p.op("pe", lambda e: e.matmul(B[:, :n], lhsT=self.ones_f[0:1, :], rhs=nmr, start=True, stop=True),
             reads=[std, self.cd], writes=[Bd])
        for c in range(8):
            t1, t1d = self.rot("lnt", 1, [128, 512], F32, st)
            p.op("dve", lambda e, c=c, t1=t1: e.tensor_tensor(out=t1[:, :n], in0=self.xres[:, c, t0:t0 + n],
                                                              in1=A[:, :n], op=ALU.mult),
                 reads=[xd, Ad], writes=[t1d])
            p.op("dve", lambda e, t1=t1: e.tensor_tensor(out=t1[:, :n], in0=t1[:, :n], in1=B[:, :n], op=ALU.add),
                 reads=[t1d, Bd], writes=[t1d])
            p.op("act", lambda e, c=c, t1=t1: e.activation(
                out=self.xres[:, c, t0:t0 + n], in_=t1[:, :n], func=AF.Identity,
                bias=self.lnb[:, lnidx * 8 + c:lnidx * 8 + c + 1], scale=self.lng[:, lnidx * 8 + c:lnidx * 8 + c + 1]),
                reads=[t1d, self.cd], writes=[xd])

    def outproj_ln(self, g, w2d, kchunks, rhs_fn, rhs_deps, lnidx, st, kp=128):
        p = self.p
        for c0 in range(0, 8):
            wt, wd = self.getw()
            wv = wt[:, 0:kchunks * 128].rearrange("p (k c) -> p k c", k=kchunks)
            p.dma(lambda e, wv=wv, c0=c0: e.dma_start(
                out=wv[:kp, :, :], in_=w2d[:, c0 * 128:(c0 + 1) * 128].rearrange("(k p) c -> p k c", p=kp)),
                writes=[wd], eng="pool")
            for cc in range(1):
                c = c0 + cc
                for tl, ti in enumerate(GROUPS[g]):
                    t0, n = TILES[ti]
                    hb, hd = self.bank()
                    for k in range(kchunks):
                        p.op("pe", lambda e, k=k, hb=hb, wv=wv, cc=cc, tl=tl, ti=ti, n=n: e.matmul(
                            hb[:, :n], lhsT=wv[:kp, k, cc * 128:(cc + 1) * 128], rhs=rhs_fn(k, tl, ti),
                            start=(k == 0), stop=(k == kchunks - 1)), reads=[wd] + rhs_deps(tl, ti), writes=[hd])
                    p.op("dve", lambda e, hb=hb, c=c, t0=t0, n=n: e.scalar_tensor_tensor(
                        out=self.xres[:, c, t0:t0 + n], in0=self.xres[:, c, t0:t0 + n], scalar=ALPHA, in1=hb[:, :n],
                        op0=ALU.mult, op1=ALU.add), reads=[hd, self.xres_d[ti]], writes=[self.xres_d[ti]])
        for ti in GROUPS[g]:
            self.layernorm(ti, lnidx, st)

    def ffn(self, l):
        p = self.p
        w_in = self.ffn_w_in[l]
        w_out = self.ffn_w_out[l]
        with ExitStack() as st:
            hid = p.sb([128, 22, 1040], BF16, st=st, name="hid")
            hid_d = [Dep() for _ in range(3)]
            for g in range(2):
                self.cast_xb(g)
                for j0 in range(0, 22, 2):
                    nj = 2
                    wt, wd = self.getw()
                    wv = wt[:, :].rearrange("p (k s c) -> p k s c", k=8, s=2)
                    for s in range(2):
                        p.dma(lambda e, wv=wv, s=s, j0=j0, nj=nj: e.dma_start(
                            out=wv[:, :, s, 0:nj * 128],
                            in_=w_in[:, s * FH + j0 * 128:s * FH + (j0 + nj) * 128].rearrange("(k p) c -> p k c", p=128)),
                            writes=[wd], eng="pool")
                    for jj in range(nj):
                        j = j0 + jj
                        for tl, ti in enumerate(GROUPS[g]):
                            t0, n = TILES[ti]
                            off = t0 - GSTART[g]
                            gb, gd = self.bank()
                            ub, ud = self.bank()
                            for s, (bk, bd) in enumerate(((gb, gd), (ub, ud))):
                                for k in range(8):
                                    p.op("pe", lambda e, bk=bk, wv=wv, k=k, s=s, jj=jj, off=off, n=n: e.matmul(
                                        bk[:, :n], lhsT=wv[:, k, s, jj * 128:(jj + 1) * 128],
                                        rhs=self.xb[:, k, off:off + n], start=(k == 0), stop=(k == 7)),
                                        reads=[wd, self.xb_d[tl]], writes=[bd])
                            sg, sgd = self.rot("sg", 2, [128, 512], F32, st)
                            p.op("act", lambda e, sg=sg, gb=gb, n=n: e.activation(out=sg[:, :n], in_=gb[:, :n], func=AF.Silu),
                                 reads=[gd], writes=[sgd])
                            p.op("dve", lambda e, sg=sg, ub=ub, j=j, off=off, n=n: e.tensor_tensor(
                                out=hid[:, j, off:off + n], in0=sg[:, :n], in1=ub[:, :n], op=ALU.mult),
                                reads=[sgd, ud], writes=[hid_d[tl]])
                self.outproj_ln(g, w_out, 22, lambda k, tl, ti, g=g: hid[:, k, TILES[ti][0] - GSTART[g]:TILES[ti][0] - GSTART[g] + TILES[ti][1]],
                                lambda tl, ti: [hid_d[tl]], l * 2 + 1, st)
            p.fence()

    def build(self):
        self.setup_consts()
        self.load_x()
        for l in range(self.layers):
            kind = l % 3
            if kind == 0:
                self.nsa(l // 3, l)
            elif kind == 1:
                self.gmlp(l)
            else:
                self.hgrn(l)
            self.ffn(l)
        self.store_y()

    def nsa(self, j, l):
        p = self.p
        w_in = self.nsa_w_in[j]
        outs_p = [self.o_cmp_p, self.o_slc_p, self.o_win_p]
        outs_s = [self.o_cmp_s, self.o_slc_s, self.o_win_s]
        with ExitStack() as st:
            nd = Dep()
            N = type("N", (), {})()
            self.N = N
            N.j, N.st, N.nd = j, st, nd
            N.qT = p.sb([128, 8, 1040], BF16, st=st, name="qT")
            N.qT_d = [Dep() for _ in range(3)]
            N.KsT = p.sb([128, 2, NT], BF16, st=st, name="KsT")
            N.KwT = p.sb([128, 2, NT], BF16, st=st, name="KwT")
            N.KcT = p.sb([128, 2, T], BF16, st=st, name="KcT")
            N.VcT = p.sb([128, 2, T], BF16, st=st, name="VcT")
            N.kT_d = [Dep() for _ in TILES]
            N.Vs = p.sb([128, 16, 4, 65], BF16, st=st, name="Vs")
            N.Vw = p.sb([128, 16, 4, 65], BF16, st=st, name="Vw")
            N.vnew = p.sb([NS, 2, 4, 65], BF16, st=st, name="vnew")
            N.V_d = [Dep() for _ in range(17)]
            N.gat = p.sb([128, 17, 48], F32, st=st, name="gat")
            N.gat_d = [Dep() for _ in range(17)]
            N.kcT = p.sb([128, 2, 128], BF16, st=st, name="kcT")
            N.vcx = p.sb([128, 4, 97], BF16, st=st, name="vcx")
            N.cmp_d = Dep()
            N.Eblk = p.sb([32, T], BF16, st=st, name="Eblk")
            N.Cbias = p.sb([128, 16, 32], F32, st=st, name="Cbias")
            N.bgate = p.sb([128, 48], F32, st=st, name="bgate")
            N.zrow = p.sb([1, 512], BF16, st=st, name="zrow")
            cmapf = p.sb([128, 32], F32, st=st)
            p.op("dve", lambda e: e.memset(N.zrow[:], 0.0), writes=[nd])
            p.op("dve", lambda e: e.memset(N.Vs[:], 1.0), writes=N.V_d)
            p.op("dve", lambda e: e.memset(N.Vw[:], 1.0), writes=N.V_d)
            p.op("dve", lambda e: e.memset(N.vnew[:], 1.0), writes=N.V_d)
            p.op("dve", lambda e: e.memset(N.vcx[:], 1.0), writes=[N.cmp_d])
            p.dma(lambda e: e.dma_start(out=N.bgate[:], in_=self.nsa_b_gate[j].partition_broadcast(128)), writes=[nd])
            p.op("dve", lambda e: e.memset(N.Eblk[:], 1.0), writes=[nd])
            p.op("pool", lambda e: e.affine_select(out=N.Eblk[:], in_=N.Eblk[:], pattern=[[1, T]], compare_op=ALU.is_ge, fill=0.0,
                                                   base=0, channel_multiplier=-64), reads=[nd], writes=[nd])
            p.op("pool", lambda e: e.affine_select(out=N.Eblk[:], in_=N.Eblk[:], pattern=[[-1, T]], compare_op=ALU.is_ge, fill=0.0,
                                                   base=63, channel_multiplier=64), reads=[nd], writes=[nd])
            with ExitStack() as s0:
                d0 = p.sb([128, 16, 32], F32, st=s0)
                f1 = p.sb([128, 16, 32], F32, st=s0)
                f2 = p.sb([128, 16, 32], F32, st=s0)
                hi = p.sb([128, 1], F32, st=s0)
                v1 = p.sb([128, 32], F32, st=s0)
                fl = lambda t: t[:].rearrange("p a b -> p (a b)")
                p.op("pool", lambda e: e.iota(fl(d0), pattern=[[-2, 16], [1, 32]], base=0, channel_multiplier=0,
                                              allow_small_or_imprecise_dtypes=True), writes=[nd])
                p.op("pool", lambda e: e.iota(hi[:], pattern=[[0, 1]], base=0, channel_multiplier=1,
                                              allow_small_or_imprecise_dtypes=True), writes=[nd])
                p.op("pool", lambda e: e.iota(fl(f2), pattern=[[0, 16], [1, 32]], base=0, channel_multiplier=0,
                                              allow_small_or_imprecise_dtypes=True), writes=[nd])
                p.op("dve", lambda e: e.tensor_single_scalar(out=hi[:], in_=hi[:], scalar=64.0, op=ALU.is_ge), reads=[nd], writes=[nd])
                p.op("dve", lambda e: e.tensor_scalar(out=fl(d0), in0=fl(d0), scalar1=hi[:, 0:1], scalar2=None, op0=ALU.subtract),
                     reads=[nd], writes=[nd])
                p.op("dve", lambda e: e.tensor_single_scalar(out=fl(f2), in_=fl(f2), scalar=0.0, op=ALU.is_equal), reads=[nd], writes=[nd])
                p.op("dve", lambda e: e.tensor_single_scalar(out=fl(f1), in_=fl(d0), scalar=0.0, op=ALU.is_equal), reads=[nd], writes=[nd])
                p.op("dve", lambda e: e.tensor_tensor(out=fl(f2), in0=fl(f2), in1=fl(f1), op=ALU.max), reads=[nd], writes=[nd])
                p.op("dve", lambda e: e.tensor_single_scalar(out=fl(f1), in_=fl(d0), scalar=-1.0, op=ALU.is_equal), reads=[nd], writes=[nd])
                p.op("dve", lambda e: e.tensor_tensor(out=fl(f2), in0=fl(f2), in1=fl(f1), op=ALU.max), reads=[nd], writes=[nd])
                p.op("dve", lambda e: e.tensor_scalar(out=fl(f1), in0=fl(d0), scalar1=0.0, scalar2=-1e9, op0=ALU.is_gt, op1=ALU.mult),
                     reads=[nd], writes=[nd])
                p.op("dve", lambda e: e.scalar_tensor_tensor(out=fl(N.Cbias), in0=fl(f2), scalar=1000.0, in1=fl(f1),
                                                             op0=ALU.mult, op1=ALU.add), reads=[nd], writes=[nd])
                p.op("pool", lambda e: e.iota(cmapf[:], pattern=[[-64, 32]], base=32, channel_multiplier=16,
                                              allow_small_or_imprecise_dtypes=True), writes=[nd])
                p.op("pool", lambda e: e.iota(v1[:], pattern=[[64, 32]], base=64, channel_multiplier=-16,
                                              allow_small_or_imprecise_dtypes=True), writes=[nd])
                p.op("dve", lambda e: e.tensor_tensor(out=cmapf[:], in0=cmapf[:], in1=v1[:], op=ALU.min), reads=[nd], writes=[nd])
                p.op("dve", lambda e: e.tensor_scalar(out=cmapf[:], in0=cmapf[:], scalar1=32.0, scalar2=0.0, op0=ALU.min, op1=ALU.max),
                     reads=[nd], writes=[nd])
                p.op("dve", lambda e: e.tensor_scalar(out=N.vcx[:, :, 65:97], in0=cmapf[:].unsqueeze(1).to_broadcast([128, 4, 32]),
                                                      scalar1=1.0 / 16, scalar2=None, op0=ALU.mult), reads=[nd, N.cmp_d], writes=[N.cmp_d])
                p.fence()
            for g in range(2):
                self.cast_xb(g)
                self.nsa_project(g, w_in, outs_p, outs_s)
                self.nsa_compress(g)
                for jt in (2 * g, 2 * g + 1):
                    self.nsa_attend(g, jt)
                if g == 1:
                    self.nsa_sample()
                with self.substack() as st2:
                    self.outproj_ln(g, self.nsa_w_out[j], 8,
                                    lambda k, tl, ti, g=g: self.xb[:, k, TILES[ti][0] - GSTART[g]:TILES[ti][0] - GSTART[g] + TILES[ti][1]],
                                    lambda tl, ti: [self.xb_d[tl]], l * 2, st2)
            p.fence()

    def nsa_project(self, g, w_in, outs_p, outs_s):
        with self.substack() as st:
            self._nsa_project(g, w_in, outs_p, outs_s, st)

    def _nsa_project(self, g, w_in, outs_p, outs_s, st):
        p, N = self.p, self.N
        j = N.j
        tiles = list(enumerate(GROUPS[g]))

        def fm(wv, nch, dest_fn, only_prompt=False):
            for ch in range(nch):
                for tl, ti in tiles:
                    t0, n = TILES[ti]
                    if only_prompt and ti == 4:
                        continue
                    off = t0 - GSTART[g]
                    bk, bd = self.bank()
                    for k in range(8):
                        p.op("pe", lambda e, bk=bk, k=k, ch=ch, off=off, n=n: e.matmul(
                            bk[:, :n], lhsT=wv[:, k, ch * 128:(ch + 1) * 128], rhs=self.xb[:, k, off:off + n],
                            start=(k == 0), stop=(k == 7)), reads=[wd, self.xb_d[tl]], writes=[bd])
                    dst, dd = dest_fn(ch, tl, ti, t0, n, off)
                    p.op("act", lambda e, bk=bk, dst=dst, n=n: e.copy(out=dst, in_=bk[:, :n]), reads=[bd], writes=[dd])

        for pair in range(2):
            wt, wd = self.getw()
            wv = wt[:, 0:4096].rearrange("p (k c) -> p k c", k=8)
            for r in range(4):
                for hf in range(2):
                    col = ((2 * pair + hf) * 4 + r) * 64
                    p.dma(lambda e, wv=wv, r=r, hf=hf, col=col: e.dma_start(
                        out=wv[:, :, r * 128 + hf * 64:r * 128 + hf * 64 + 64],
                        in_=w_in[:, col:col + 64].rearrange("(k p) c -> p k c", p=128)), writes=[wd], eng="pool")
            fm(wv, 4, lambda ch, tl, ti, t0, n, off, pair=pair: (N.qT[:, pair * 4 + ch, off:off + n], N.qT_d[tl]))
        wt, wd = self.getw()
        wv = wt[:, 0:4096].rearrange("p (k c) -> p k c", k=8)
        p.dma(lambda e, wv=wv: e.dma_start(out=wv, in_=w_in[:, 1024:1536].rearrange("(k p) c -> p k c", p=128)), writes=[wd], eng="pool")
        fm(wv, 4, lambda ch, tl, ti, t0, n, off: ((N.KcT if ch < 2 else N.VcT)[:, ch % 2, t0:t0 + n], N.kT_d[ti]), only_prompt=True)
        wt, wd = self.getw()
        wv = wt[:, 0:4096].rearrange("p (k c) -> p k c", k=8)
        p.dma(lambda e, wv=wv: e.dma_start(out=wv[:, :, 0:256], in_=w_in[:, 1536:1792].rearrange("(k p) c -> p k c", p=128)),
              writes=[wd], eng="pool")
        p.dma(lambda e, wv=wv: e.dma_start(out=wv[:, :, 256:512], in_=w_in[:, 2048:2304].rearrange("(k p) c -> p k c", p=128)),
              writes=[wd], eng="pool")
        fm(wv, 4, lambda ch, tl, ti, t0, n, off: ((N.KsT if ch < 2 else N.KwT)[:, ch % 2, t0:t0 + n], N.kT_d[ti]))
        for br in range(4):
            wt, wd = self.getw()
            wv = wt[:, 0:4096].rearrange("p (k c) -> p k c", k=8)
            wdt = 512 if br < 3 else 48
            c0 = 1024 + br * 512
            p.dma(lambda e, wv=wv, c0=c0, wdt=wdt: e.dma_start(
                out=wv[:, :, 0:wdt], in_=w_in[:, c0:c0 + wdt].rearrange("(k p) c -> p k c", p=128)), writes=[wd], eng="pool")
            for tl, ti in tiles:
                t0, n = TILES[ti]
                for b0 in range(0, n, 128):
                    nb = min(128, n - b0)
                    off = t0 - GSTART[g] + b0
                    tok = t0 + b0
                    blk = tok // 128
                    bk, bd = self.bank()
                    for k in range(8):
                        p.op("pe", lambda e, bk=bk, wv=wv, k=k, off=off, nb=nb, wdt=wdt: e.matmul(
                            bk[:nb, 0:wdt], lhsT=self.xb[:, k, off:off + nb], rhs=wv[:, k, 0:wdt], start=(k == 0), stop=(k == 7)),
                            reads=[wd, self.xb_d[tl]], writes=[bd])
                    if br == 3:
                        p.op("dve", lambda e, bk=bk, blk=blk, nb=nb: e.tensor_tensor(out=N.gat[:nb, blk, :], in0=bk[:nb, 0:48],
                                                                                    in1=N.bgate[:nb, :], op=ALU.add),
                             reads=[bd, N.nd], writes=[N.gat_d[blk]])
                        p.op("act", lambda e, blk=blk, nb=nb: e.activation(out=N.gat[:nb, blk, :], in_=N.gat[:nb, blk, :], func=AF.Sigmoid),
                             reads=[N.gat_d[blk]], writes=[N.gat_d[blk]])
                        continue
                    sg, sgd = self.rot("kvst", 2, [128, 512], F32, st)
                    p.op("act", lambda e, sg=sg, bk=bk, nb=nb: e.copy(out=sg[:nb, :], in_=bk[:nb, :]), reads=[bd], writes=[sgd])
                    if ti == 4:
                        p.dma(lambda e, sg=sg, br=br: e.dma_start(out=outs_s[br][j, :, :], in_=sg[:NS, :]), reads=[sgd])
                        if br >= 1:
                            p.op("pool", lambda e, sg=sg, br=br: e.tensor_copy(
                                out=N.vnew[:, br - 1, :, 0:64], in_=sg[:NS, 256:512].rearrange("p (a b) -> p a b", a=4)),
                                reads=[sgd], writes=[N.V_d[16]])
                    else:
                        if br < 2:
                            p.dma(lambda e, sg=sg, br=br, tok=tok: e.dma_start(out=outs_p[br][j, tok:tok + 128, :], in_=sg[:, :]), reads=[sgd])
                        elif tok >= T - 512:
                            p.dma(lambda e, sg=sg, tok=tok: e.dma_start(out=self.o_win_p[j, tok - (T - 512):tok - (T - 512) + 128, :],
                                                                        in_=sg[:, :]), reads=[sgd])
                        if br >= 1:
                            Vt = N.Vs if br == 1 else N.Vw
                            p.op("pool", lambda e, sg=sg, Vt=Vt, blk=blk: e.tensor_copy(
                                out=Vt[:, blk, :, 0:64], in_=sg[:, 256:512].rearrange("p (a b) -> p a b", a=4)),
                                reads=[sgd], writes=[N.V_d[blk]])

    def compress_weights(self, Wk, Wv, wdp, st):
        p, N = self.p, self.N
        j = N.j
        peT = p.sb([128, 2, 32], BF16, st=st, name="peT")
        biask = p.sb([128, 1], F32, st=st)
        bvrow = p.sb([1, 128], BF16, st=st)
        p.op("dve", lambda e: e.memset(Wk, 0.0), writes=[wdp])
        p.op("pool", lambda e: e.memset(Wv, 0.0), writes=[wdp])
        for kv, W in ((0, Wk), (1, Wv)):
            for hf in range(2):
                p.dma(lambda e, kv=kv, W=W, hf=hf: e.dma_start(
                    out=W[hf * 64:(hf + 1) * 64, :, hf * 64:(hf + 1) * 64],
                    in_=self.nsa_w_cmp[j, kv].rearrange("p d e -> d p e")), writes=[wdp], eng="pool")
                p.dma(lambda e, kv=kv, hf=hf: e.dma_start(
                    out=peT[hf * 64:(hf + 1) * 64, kv, :], in_=self.nsa_pe_cmp[j, kv].rearrange("p d -> d p"),
                    allow_slow_non_contiguous=True), writes=[wdp], eng="pool")
        bk, bd = self.bank()
        for pp in range(32):
            p.op("pe", lambda e, pp=pp: e.matmul(bk[:, 0:1], lhsT=Wk[:, pp, :], rhs=peT[:, 0, pp:pp + 1], start=(pp == 0), stop=(pp == 31)),
                 reads=[wdp], writes=[bd])
        p.op("act", lambda e: e.copy(out=biask[:], in_=bk[:, 0:1]), reads=[bd], writes=[wdp])
        bk2, bd2 = self.bank()
        for pp in range(32):
            p.op("pe", lambda e, pp=pp: e.matmul(bk2[0:1, 0:128], lhsT=peT[:, 1, pp:pp + 1], rhs=Wv[:, pp, :], start=(pp == 0), stop=(pp == 31)),
                 reads=[wdp], writes=[bd2])
        p.op("act", lambda e: e.copy(out=bvrow[:], in_=bk2[0:1, 0:128]), reads=[bd2], writes=[wdp])
        return Wk, Wv, biask, bvrow, wdp

    def compress_run(self, cw, KcT, VcT, kdeps, nblk):
        p, N = self.p, self.N
        Wk, Wv, biask, bvrow, wdp = cw
        last = 16 * (nblk - 1)
        for pair in range(2):
            kb_, kbd = self.bank()
            for pp in range(32):
                p.op("pe", lambda e, pp=pp, pair=pair, kb_=kb_: e.matmul(
                    kb_[:, 0:nblk], lhsT=Wk[:, pp, :], rhs=KcT[:, pair, pp:pp + last + 1:16], start=(pp == 0), stop=(pp == 31)),
                    reads=[wdp] + kdeps, writes=[kbd])
            p.op("act", lambda e, pair=pair, kb_=kb_: e.activation(out=N.kcT[:, pair, 0:nblk], in_=kb_[:, 0:nblk], func=AF.Identity,
                                                                  bias=biask[:, 0:1], scale=1.0),
                 reads=[kbd, wdp], writes=[N.cmp_d])
        vb_, vbd = self.bank()
        for pair in range(2):
            for pp in range(32):
                p.op("pe", lambda e, pp=pp, pair=pair: e.matmul(
                    vb_[0:nblk, pair * 128:(pair + 1) * 128], lhsT=VcT[:, pair, pp:pp + last + 1:16], rhs=Wv[:, pp, :],
                    start=(pp == 0), stop=False), reads=[wdp] + kdeps, writes=[vbd])
            p.op("pe", lambda e, pair=pair: e.matmul(vb_[0:nblk, pair * 128:(pair + 1) * 128], lhsT=self.ones_bf[0:1, 0:nblk],
                                                     rhs=bvrow[0:1, :], start=False, stop=True), reads=[wdp, self.cd], writes=[vbd])
        p.op("act", lambda e: e.copy(out=N.vcx[0:nblk, :, 0:64], in_=vb_[0:nblk, 0:256].rearrange("p (a b) -> p a b", a=4)),
             reads=[vbd], writes=[N.cmp_d])

    def nsa_compress(self, g):
        p, N = self.p, self.N
        with self.substack() as s0:
            Wk = p.sb([128, 32, 128], BF16, st=s0, name="Wk")
            Wv = p.sb([128, 32, 128], BF16, st=s0, name="Wv")
            cw = self.compress_weights(Wk[:], Wv[:], Dep(), s0)
            self.compress_run(cw, N.KcT, N.VcT, [N.kT_d[ti] for ti in range(2 * g + 2)], 63 if g == 0 else 127)

    def nsa_attend(self, g, jt):
        with self.substack() as st:
            self._nsa_attend(g, jt, st)

    def _nsa_attend(self, g, jt, st):
        p, N = self.p, self.N
        q0 = 512 * jt
        qoff = q0 - GSTART[g]
        tl = jt - 2 * g
        blk0 = q0 // 128
        nblk = 63 if g == 0 else 127
        nkb = (q0 + 512) // 128
        kdeps = [N.kT_d[ti] for ti in range(jt + 1)]
        for gg in range(4):
            pair, half = gg // 2, gg % 2
            P = slice(64 * half, 64 * half + 64)
            oacc, od = self.rot("oacc", 1, [128, 4, 256], F32, st)
            imp, impd = self.rot("imp", 1, [128, 4, 32], F32, st)

            def finish_head(acc, accd, ai, width, r, gcol, first, with_imp=False):
                a3 = acc[:, 0:4 * width].rearrange("p (q c) -> p q c", c=width)
                rs, rsd = self.rot("rs", 2, [128, 4], F32, st)
                p.op("dve", lambda e: e.tensor_scalar(out=rs[:, :], in0=a3[:, :, 64], scalar1=1e-30, scalar2=None, op0=ALU.add),
                     reads=[accd], writes=[rsd])
                p.op("dve", lambda e: e.reciprocal(out=rs[:, :], in_=rs[:, :]), reads=[rsd], writes=[rsd])
                if with_imp:
                    if r == 0:
                        p.op("dve", lambda e: e.tensor_tensor(out=imp[:, :, :], in0=a3[:, :, 65:97],
                                                              in1=rs[:, :].unsqueeze(2).to_broadcast([128, 4, 32]), op=ALU.mult),
                             reads=[accd, rsd], writes=[impd])
                    else:
                        ti_, tid = self.rot("impt", 1, [128, 4, 32], F32, st)
                        p.op("dve", lambda e: e.tensor_tensor(out=ti_[:, :, :], in0=a3[:, :, 65:97],
                                                              in1=rs[:, :].unsqueeze(2).to_broadcast([128, 4, 32]), op=ALU.mult),
                             reads=[accd, rsd], writes=[tid])
                        p.op("dve", lambda e: e.tensor_tensor(out=imp[:, :, :], in0=imp[:, :, :], in1=ti_[:, :, :], op=ALU.add),
                             reads=[tid, impd], writes=[impd])
                p.op("dve", lambda e: e.tensor_tensor(out=rs[:, :], in0=rs[:, :], in1=N.gat[:, blk0:blk0 + 4, gcol], op=ALU.mult),
                     reads=[rsd] + N.gat_d[blk0:blk0 + 4], writes=[rsd])
                dst = oacc[:, :, r * 64:(r + 1) * 64]
                if first:
                    p.op("dve", lambda e: e.tensor_tensor(out=dst, in0=a3[:, :, 0:64], in1=rs[:, :].unsqueeze(2).to_broadcast([128, 4, 64]),
                                                          op=ALU.mult), reads=[accd, rsd], writes=[od])
                else:
                    to, tod = self.rot("otmp", 1, [128, 4, 64], F32, st)
                    p.op("dve", lambda e: e.tensor_tensor(out=to[:, :, :], in0=a3[:, :, 0:64],
                                                          in1=rs[:, :].unsqueeze(2).to_broadcast([128, 4, 64]), op=ALU.mult),
                         reads=[accd, rsd], writes=[tod])
                    p.op("dve", lambda e: e.tensor_tensor(out=dst, in0=dst, in1=to[:, :, :], op=ALU.add), reads=[tod, od], writes=[od])
                self.release(ai)

            def new_acc(width):
                acc, accd, ai = self.bank(hold=True)
                p.op("pe", lambda e: e.matmul(acc[:, 0:4 * width], lhsT=N.zrow[0:1, 0:128], rhs=N.zrow[0:1, 0:4 * width],
                                              start=True, stop=False), reads=[N.nd], writes=[accd])
                return acc, accd, ai

            for r in range(4):
                h = 4 * gg + r
                rq = N.qT[P, pair * 4 + r, qoff:qoff + 512]
                acc, accd, ai = new_acc(97)
                S, Sd = self.bank()
                p.op("pe", lambda e: e.matmul(S[0:nblk, :], lhsT=N.kcT[P, pair, 0:nblk], rhs=rq, start=True, stop=True),
                     reads=[N.cmp_d, N.qT_d[tl]], writes=[Sd])
                ET, ETd = self.rot("ET", 3, [128, 512], BF16, st)
                p.op("act", lambda e: e.activation(out=ET[0:nblk, :], in_=S[0:nblk, :], func=AF.Exp, scale=SCALE), reads=[Sd], writes=[ETd])
                p.op("pool", lambda e: e.affine_select(out=ET[0:nblk, :], in_=ET[0:nblk, :], pattern=[[1, 512]], compare_op=ALU.is_ge,
                                                       fill=0.0, base=q0 - 31, channel_multiplier=-16), reads=[ETd], writes=[ETd])
                for qb in range(4):
                    p.op("pe", lambda e, qb=qb: e.matmul(acc[:, qb * 97:(qb + 1) * 97], lhsT=ET[0:nblk, qb * 128:(qb + 1) * 128],
                                                         rhs=N.vcx[0:nblk, gg, :], start=False, stop=(qb == 3)),
                         reads=[ETd, N.cmp_d], writes=[accd])
                finish_head(acc, accd, ai, 97, r, h, True, with_imp=True)
            tb, tbd = self.bank()
            for qb in range(4):
                sc, scd = self.rot("sc", 2, [128, 3, 32], F32, st)
                m8, m8d = self.rot("m8", 2, [128, 2, 8], F32, st)
                p.op("dve", lambda e, qb=qb: e.tensor_tensor(out=sc[:, 0, :], in0=imp[:, qb, :], in1=N.Cbias[:, blk0 + qb, :], op=ALU.add),
                     reads=[impd, N.nd], writes=[scd])
                p.op("dve", lambda e: e.max(out=m8[:, 0, :], in_=sc[:, 0, :]), reads=[scd], writes=[m8d])
                p.op("dve", lambda e: e.match_replace(out=sc[:, 1, :], in_to_replace=m8[:, 0, :], in_values=sc[:, 0, :], imm_value=-1e30),
                     reads=[scd, m8d], writes=[scd])
                p.op("dve", lambda e: e.max(out=m8[:, 1, :], in_=sc[:, 1, :]), reads=[scd, m8d], writes=[m8d])
                p.op("dve", lambda e: e.tensor_scalar(out=sc[:, 2, :], in0=sc[:, 0, :], scalar1=m8[:, 1, 7:8], scalar2=None, op0=ALU.is_ge),
                     reads=[scd, m8d], writes=[scd])
                p.op("dve", lambda e: e.tensor_scalar(out=sc[:, 2, :], in0=sc[:, 2, :], scalar1=-1.0, scalar2=-NEG, op0=ALU.add, op1=ALU.mult),
                     reads=[scd], writes=[scd])
                p.op("pe", lambda e, qb=qb: e.transpose(out=tb[0:32, qb * 128:(qb + 1) * 128], in_=sc[:, 2, :], identity=self.ident_f[:, :]),
                     reads=[scd, self.cd], writes=[tbd])
            selT, seld = self.rot("selT", 2, [32, 512], BF16, st)
            p.op("act", lambda e: e.copy(out=selT[:, :], in_=tb[0:32, :]), reads=[tbd], writes=[seld])
            for bi, (KT, V, gbase) in enumerate(((N.KsT, N.Vs, 16), (N.KwT, N.Vw, 32))):
                kb0 = 0 if bi == 0 else max(0, q0 - 512) // 128
                for r in range(4):
                    h = 4 * gg + r
                    rq = N.qT[P, pair * 4 + r, qoff:qoff + 512]
                    acc, accd, ai = new_acc(65)
                    for kb in range(kb0, nkb):
                        k0 = kb * 128
                        S, Sd = self.bank()
                        p.op("pe", lambda e, k0=k0: e.matmul(S[:, :], lhsT=KT[P, pair, k0:k0 + 128], rhs=rq, start=True, stop=(bi == 1)),
                             reads=kdeps + [N.qT_d[tl]], writes=[Sd])
                        if bi == 0:
                            p.op("pe", lambda e, k0=k0: e.matmul(S[:, :], lhsT=N.Eblk[0:32, k0:k0 + 128], rhs=selT[0:32, :], start=False, stop=True),
                                 reads=[N.nd, seld], writes=[Sd])
                        ET, ETd = self.rot("ET", 3, [128, 512], BF16, st)
                        p.op("act", lambda e: e.activation(out=ET[:, :], in_=S[:, :], func=AF.Exp, scale=SCALE), reads=[Sd], writes=[ETd])
                        if k0 + 127 > q0:
                            p.op("pool", lambda e, k0=k0: e.affine_select(out=ET[:, :], in_=ET[:, :], pattern=[[1, 512]], compare_op=ALU.is_ge,
                                                                          fill=0.0, base=q0 - k0, channel_multiplier=-1),
                                 reads=[ETd], writes=[ETd])
                        if bi == 1 and k0 < q0:
                            p.op("pool", lambda e, k0=k0: e.affine_select(out=ET[:, :], in_=ET[:, :], pattern=[[-1, 512]], compare_op=ALU.is_ge,
                                                                          fill=0.0, base=k0 - q0 + 511, channel_multiplier=1),
                                 reads=[ETd], writes=[ETd])
                        for qb in range(4):
                            qq = q0 + qb * 128
                            if k0 > qq + 127:
                                continue
                            if bi == 1 and k0 + 127 <= qq - 512:
                                continue
                            p.op("pe", lambda e, qb=qb, kb=kb: e.matmul(acc[:, qb * 65:(qb + 1) * 65], lhsT=ET[:, qb * 128:(qb + 1) * 128],
                                                                        rhs=V[:, kb, gg, :], start=False, stop=False),
                                 reads=[ETd, N.V_d[kb]], writes=[accd])
                    finish_head(acc, accd, ai, 65, r, gbase + h, False)
            for i in range(2):
                tb, tbd = self.bank()
                for qb in range(4):
                    p.op("pe", lambda e, qb=qb, i=i: e.transpose(out=tb[:, qb * 128:(qb + 1) * 128], in_=oacc[:, qb, i * 128:(i + 1) * 128],
                                                                 identity=self.ident_f[:, :]), reads=[od, self.cd], writes=[tbd])
                p.op("act", lambda e, i=i: e.copy(out=self.xb[:, 2 * gg + i, qoff:qoff + 512], in_=tb[:, :]),
                     reads=[tbd], writes=[self.xb_d[tl]])

    def nsa_sample(self):
        p, N = self.p, self.N
        j = N.j
        XO = 1024
        with self.substack() as st:
            sd = Dep()
            KT, VT, Vs, Vw = N.KcT, N.VcT, N.Vs, N.Vw
            kd, vd = Dep(), Dep()
            Wk = self.wbuf[0][:, :].rearrange("p (a b) -> p a b", a=32)
            Wv = self.wbuf[1][:, :].rearrange("p (a b) -> p a b", a=32)
            wdp = Dep()
            p.fence()
            cw = self.compress_weights(Wk, Wv, wdp, st)
            ptb = p.sb([128, 256], I32, st=st)
            ptf = p.sb([128, 256], F32, st=st)
            io = p.sb([128, 1], F32, st=st)
            idx = p.sb([128, 256], U32, st=st)
            sbias = p.sb([1, 32], F32, st=st)
            one1 = p.sb([1, 16], F32, st=st)
            p.dma(lambda e: e.dma_start(out=ptb[:], in_=self.page_table.rearrange("a b -> (a b)").partition_broadcast(128)), writes=[sd])
            p.op("pool", lambda e: e.iota(io[:], pattern=[[0, 1]], base=j * self.n_phys * 128, channel_multiplier=1,
                                          allow_small_or_imprecise_dtypes=True), writes=[sd])
            p.op("dve", lambda e: e.tensor_copy(out=ptf[:], in_=ptb[:]), reads=[sd], writes=[sd])
            p.op("dve", lambda e: e.tensor_scalar(out=ptf[:], in0=ptf[:], scalar1=128.0, scalar2=io[:, 0:1], op0=ALU.mult, op1=ALU.add),
                 reads=[sd], writes=[sd])
            p.op("dve", lambda e: e.tensor_copy(out=idx[:], in_=ptf[:]), reads=[sd], writes=[sd])
            p.op("dve", lambda e: e.memset(sbias[:], 0.0), writes=[sd])
            p.op("dve", lambda e: e.memset(sbias[0:1, 0:1], 1000.0), reads=[sd], writes=[sd])
            p.op("dve", lambda e: e.memset(sbias[0:1, 31:32], 1000.0), reads=[sd], writes=[sd])
            p.op("dve", lambda e: e.memset(one1[:], 1.0), writes=[sd])
            ind = p.sb([128, 2], F32, st=st)
            p.op("dve", lambda e: e.memset(ind[:], 0.0), writes=[sd])
            p.op("dve", lambda e: e.memset(ind[0:64, 0:1], 1.0), reads=[sd], writes=[sd])
            p.op("dve", lambda e: e.memset(ind[64:128, 1:2], 1.0), reads=[sd], writes=[sd])
            ccmp = self.cache_cmp
            cslc = self.cache_slc
            ocol, ocd, oci = self.bank(hold=True)
            p.op("pe", lambda e: e.matmul(ocol[:, 0:128], lhsT=N.zrow[0:1, 0:128], rhs=N.zrow[0:1, 0:128], start=True, stop=True),
                 reads=[N.nd], writes=[ocd])
            import os
            STOP = int(os.environ.get("NSA_STOP", "99"))
            NSB = int(os.environ.get("NSA_NSB", str(NS)))
            for b in range(NSB):
                qcol = XO + b
                for pg in range(16):
                    pt_, ptd = self.rot("pg", 2, [128, 512], F32, st)
                    if os.environ.get("NSA_PLAIN"):
                        p.dma(lambda e, pt_=pt_, b=b, pg=pg: e.dma_start(out=pt_[:, :], in_=ccmp[pg * 128:(pg + 1) * 128, :]),
                              reads=[sd], writes=[ptd], eng="pool")
                    else:
                        p.dma(lambda e, pt_=pt_, b=b, pg=pg: e.indirect_dma_start(
                            out=pt_[:, :], out_offset=None, in_=ccmp,
                            in_offset=bass.IndirectOffsetOnAxis(ap=idx[:, b * 16 + pg:b * 16 + pg + 1], axis=0)),
                            reads=[sd], writes=[ptd], eng="pool")
                    SK = int(os.environ.get("NSA_SKIP", "0"))
                    if SK >= 3:
                        continue
                    tb, tbd = self.bank()
                    for i in range(4):
                        p.op("pe", lambda e, tb=tb, pt_=pt_, i=i: e.transpose(out=tb[:, i * 128:(i + 1) * 128], in_=pt_[:, i * 128:(i + 1) * 128],
                                                                             identity=self.ident_f[:, :]), reads=[ptd, self.cd], writes=[tbd])
                    if SK >= 2:
                        continue
                    p.op("act", lambda e, tb=tb, pg=pg: e.copy(out=KT[:, :, pg * 128:(pg + 1) * 128],
                                                               in_=tb[:, 0:256].rearrange("p (a b) -> p a b", a=2)), reads=[tbd], writes=[kd])
                    if SK >= 1:
                        continue
                    p.op("act", lambda e, tb=tb, pg=pg: e.copy(out=VT[:, :, pg * 128:(pg + 1) * 128],
                                                               in_=tb[:, 256:512].rearrange("p (a b) -> p a b", a=2)), reads=[tbd], writes=[vd])
                if STOP < 1:
                    continue
                self.compress_run(cw, KT, VT, [kd, vd], 127)
                if STOP < 2:
                    continue
                gb, gbd = self.bank()
                p.op("pe", lambda e, b=b: e.matmul(gb[0:1, 0:48], lhsT=self.ident_f[0:NS, b:b + 1], rhs=N.gat[0:NS, 16, :], start=True, stop=True),
                     reads=[N.gat_d[16], self.cd], writes=[gbd])
                gs, gsd = self.rot("gs", 2, [1, 48], F32, st)
                p.op("act", lambda e, gs=gs, gb=gb: e.copy(out=gs[:, :], in_=gb[0:1, 0:48]), reads=[gbd], writes=[gsd])
                vn, vnd = self.rot("vn", 2, [1, 2, 4, 65], F32, st)
                for i in range(2):
                    vb, vbd = self.bank()
                    p.op("pe", lambda e, b=b, i=i, vb=vb: e.matmul(vb[0:1, 0:260], lhsT=self.ident_bf[0:NS, b:b + 1],
                                                                   rhs=N.vnew[0:NS, i, :, :].rearrange("p a b -> p (a b)"), start=True, stop=True),
                         reads=[N.V_d[16], self.cd], writes=[vbd])
                    p.op("act", lambda e, vn=vn, vb=vb, i=i: e.copy(out=vn[:, i, :, :].rearrange("p a b -> p (a b)"), in_=vb[0:1, 0:260]),
                         reads=[vbd], writes=[vnd])
                orow, ord_ = self.rot("orow", 1, [1, 16, 64], F32, st)

                def combine(A, Ad, width, gofs, first, orow=orow, ord_=ord_, gs=gs, gsd=gsd):
                    rs, rsd = self.rot("srs", 2, [1, 16], F32, st)
                    p.op("dve", lambda e: e.tensor_scalar(out=rs[:, :], in0=A[:, :, 64], scalar1=1e-30, scalar2=None, op0=ALU.add),
                         reads=[Ad], writes=[rsd])
                    p.op("dve", lambda e: e.reciprocal(out=rs[:, :], in_=rs[:, :]), reads=[rsd], writes=[rsd])
                    gr, grd = self.rot("sgr", 2, [1, 16], F32, st)
                    p.op("dve", lambda e: e.tensor_tensor(out=gr[:, :], in0=rs[:, :], in1=gs[0:1, gofs:gofs + 16], op=ALU.mult),
                         reads=[rsd, gsd], writes=[grd])
                    if first:
                        p.op("dve", lambda e: e.tensor_tensor(out=orow[:, :, :], in0=A[:, :, 0:64],
                                                              in1=gr[:, :].unsqueeze(2).to_broadcast([1, 16, 64]), op=ALU.mult),
                             reads=[Ad, grd], writes=[ord_])
                    else:
                        p.op("dve", lambda e: e.tensor_tensor(out=A[:, :, 0:64], in0=A[:, :, 0:64],
                                                              in1=gr[:, :].unsqueeze(2).to_broadcast([1, 16, 64]), op=ALU.mult),
                             reads=[Ad, grd], writes=[Ad])
                        p.op("dve", lambda e: e.tensor_tensor(out=orow[:, :, :], in0=orow[:, :, :], in1=A[:, :, 0:64], op=ALU.add),
                             reads=[Ad, ord_], writes=[ord_])
                    return rs, rsd

                if STOP < 3:
                    continue
                qg = lambda gg: N.qT[64 * (gg % 2):64 * (gg % 2) + 64, (gg // 2) * 4:(gg // 2) * 4 + 4, qcol]
                PP = lambda gg: slice(64 * (gg % 2), 64 * (gg % 2) + 64)
                S, Sd = self.bank()
                for gg in range(4):
                    p.op("pe", lambda e, gg=gg: e.matmul(S[0:127, 4 * gg:4 * gg + 4], lhsT=N.kcT[PP(gg), gg // 2, 0:127], rhs=qg(gg),
                                                         start=True, stop=True), reads=[N.cmp_d, N.qT_d[2]], writes=[Sd])
                Ec, Ecd = self.rot("sEc", 2, [128, 16], BF16, st)
                p.op("act", lambda e: e.activation(out=Ec[0:127, :], in_=S[0:127, 0:16], func=AF.Exp, scale=SCALE), reads=[Sd], writes=[Ecd])
                Ac, Acd = self.rot("sAc", 1, [1, 16, 97], F32, st)
                for gi in range(4):
                    ab, abd = self.bank()
                    for r in range(4):
                        h = 4 * gi + r
                        p.op("pe", lambda e, ab=ab, r=r, h=h, gi=gi: e.matmul(ab[0:1, r * 97:(r + 1) * 97], lhsT=Ec[0:127, h:h + 1],
                                                                              rhs=N.vcx[0:127, gi, :], start=True, stop=True),
                             reads=[Ecd, N.cmp_d], writes=[abd])
                    p.drain("pe")
                    p.op("act", lambda e, ab=ab, gi=gi: e.copy(out=Ac[:, 4 * gi:4 * gi + 4, :].rearrange("p a b -> p (a b)"), in_=ab[0:1, 0:388]),
                         reads=[abd], writes=[Acd])
                rs, rsd = combine(Ac, Acd, 97, 0, True)
                if STOP < 4:
                    continue
                it_, itd = self.rot("sit", 1, [1, 16, 32], F32, st)
                sc, scd = self.rot("ssc", 1, [1, 4, 3, 32], F32, st)
                m8, m8d = self.rot("sm8", 1, [1, 4, 2, 8], F32, st)
                p.op("dve", lambda e: e.tensor_tensor(out=it_[:, :, :], in0=Ac[:, :, 65:97], in1=rs[:, :].unsqueeze(2).to_broadcast([1, 16, 32]),
                                                      op=ALU.mult), reads=[Acd, rsd], writes=[itd])
                tbs, tbsd = self.bank()
                for gg in range(4):
                    p.op("dve", lambda e, gg=gg: e.tensor_tensor(out=sc[:, gg, 0, :], in0=it_[:, 4 * gg, :], in1=sbias[:, :], op=ALU.add),
                         reads=[itd, sd], writes=[scd])
                    for r in range(1, 4):
                        p.op("dve", lambda e, gg=gg, r=r: e.tensor_tensor(out=sc[:, gg, 0, :], in0=sc[:, gg, 0, :], in1=it_[:, 4 * gg + r, :],
                                                                          op=ALU.add), reads=[itd, scd], writes=[scd])
                    p.op("dve", lambda e, gg=gg: e.max(out=m8[:, gg, 0, :], in_=sc[:, gg, 0, :]), reads=[scd], writes=[m8d])
                    p.op("dve", lambda e, gg=gg: e.match_replace(out=sc[:, gg, 1, :], in_to_replace=m8[:, gg, 0, :], in_values=sc[:, gg, 0, :],
                                                                 imm_value=-1e30), reads=[scd, m8d], writes=[scd])
                    p.op("dve", lambda e, gg=gg: e.max(out=m8[:, gg, 1, :], in_=sc[:, gg, 1, :]), reads=[scd, m8d], writes=[m8d])
                    p.op("dve", lambda e, gg=gg: e.tensor_scalar(out=sc[:, gg, 2, :], in0=sc[:, gg, 0, :], scalar1=m8[:, gg, 1, 6:7], scalar2=None,
                                                                 op0=ALU.is_ge), reads=[scd, m8d], writes=[scd])
                    p.op("dve", lambda e, gg=gg: e.tensor_scalar(out=sc[:, gg, 2, :], in0=sc[:, gg, 2, :], scalar1=-1.0, scalar2=-NEG,
                                                                 op0=ALU.add, op1=ALU.mult), reads=[scd], writes=[scd])
                    p.op("pe", lambda e, gg=gg: e.matmul(tbs[0:32, 4 * gg:4 * gg + 4], lhsT=sc[0:1, gg, 2, :], rhs=one1[0:1, 0:4],
                                                         start=True, stop=True), reads=[scd, sd], writes=[tbsd])
                selT, seld = self.rot("sselT", 2, [32, 16], BF16, st)
                p.op("act", lambda e: e.copy(out=selT[:, :], in_=tbs[0:32, 0:16]), reads=[tbsd], writes=[seld])
                if STOP < 5:
                    continue
                for pg in range(16):
                    pt_, ptd = self.rot("pg", 2, [128, 512], F32, st)
                    p.dma(lambda e, pt_=pt_, b=b, pg=pg: e.indirect_dma_start(
                        out=pt_[:, :], out_offset=None, in_=cslc,
                        in_offset=bass.IndirectOffsetOnAxis(ap=idx[:, b * 16 + pg:b * 16 + pg + 1], axis=0)),
                        reads=[sd], writes=[ptd], eng="pool")
                    tb, tbd = self.bank()
                    for i in range(2):
                        p.op("pe", lambda e, tb=tb, pt_=pt_, i=i: e.transpose(out=tb[:, i * 128:(i + 1) * 128], in_=pt_[:, i * 128:(i + 1) * 128],
                                                                             identity=self.ident_f[:, :]), reads=[ptd, self.cd], writes=[tbd])
                    p.op("act", lambda e, tb=tb, pg=pg: e.copy(out=KT[:, :, pg * 128:(pg + 1) * 128],
                                                               in_=tb[:, 0:256].rearrange("p (a b) -> p a b", a=2)), reads=[tbd], writes=[kd])
                    p.op("act", lambda e, pt_=pt_, pg=pg: e.copy(out=Vs[:, pg, :, 0:64],
                                                                 in_=pt_[:, 256:512].rearrange("p (a b) -> p a b", a=4)),
                         reads=[ptd], writes=[vd])
                for wb_ in range(4):
                    pt_, ptd = self.rot("pg", 2, [128, 512], F32, st)
                    p.dma(lambda e, pt_=pt_, b=b, wb_=wb_: e.dma_start(out=pt_[:, :], in_=self.cache_win[j, b, wb_ * 128:(wb_ + 1) * 128, :]),
                          writes=[ptd])
                    tb, tbd = self.bank()
                    for i in range(2):
                        p.op("pe", lambda e, tb=tb, pt_=pt_, i=i: e.transpose(out=tb[:, i * 128:(i + 1) * 128], in_=pt_[:, i * 128:(i + 1) * 128],
                                                                             identity=self.ident_f[:, :]), reads=[ptd, self.cd], writes=[tbd])
                    p.op("act", lambda e, tb=tb, wb_=wb_: e.copy(out=VT[:, :, wb_ * 128:(wb_ + 1) * 128],
                                                                 in_=tb[:, 0:256].rearrange("p (a b) -> p a b", a=2)), reads=[tbd], writes=[vd])
                    p.op("act", lambda e, pt_=pt_, wb_=wb_: e.copy(out=Vw[:, wb_, :, 0:64],
                                                                   in_=pt_[:, 256:512].rearrange("p (a b) -> p a b", a=4)),
                         reads=[ptd], writes=[vd])
                if STOP < 6:
                    continue
                for bi, (KTs, Vx, nb_, Knew, gofs) in enumerate(((KT, Vs, 16, N.KsT, 16), (VT, Vw, 4, N.KwT, 32))):
                    if STOP < 7 + bi:
                        continue
                    S, Sd = self.bank()
                    if bi == 0:
                        Sb, Sbd = self.bank()
                        for pg in range(nb_):
                            p.op("pe", lambda e, pg=pg: e.matmul(Sb[:, pg * 16:(pg + 1) * 16], lhsT=N.Eblk[0:32, pg * 128:(pg + 1) * 128],
                                                                 rhs=selT[0:32, :], start=True, stop=True), reads=[N.nd, seld], writes=[Sbd])
                            if pg % 4 == 3:
                                p.drain("pe")
                        bS, bSd = self.rot("sbS", 1, [128, 256], F32, st)
                        p.op("act", lambda e: e.copy(out=bS[:, :], in_=Sb[:, 0:256]), reads=[Sbd], writes=[bSd])
                    for pg in range(int(os.environ.get("NSA_NPG", nb_))):
                        for gg in range(4):
                            p.op("pe", lambda e, pg=pg, gg=gg: e.matmul(
                                S[:, pg * 16 + 4 * gg:pg * 16 + 4 * gg + 4],
                                lhsT=(N.KsT if os.environ.get("NSA_E4") else KTs)[PP(gg), gg // 2, pg * 128:(pg + 1) * 128], rhs=qg(gg),
                                start=True, stop=True), reads=[kd, vd, N.qT_d[2]], writes=[Sd])
                        p.drain("pe")
                    Sn, Snd = self.bank()
                    pr, prd = self.rot("spr", 2, [128, 8], F32, st)
                    for pair in range(2):
                        p.op("dve", lambda e, pair=pair: e.tensor_tensor(
                            out=pr[:, pair * 4:(pair + 1) * 4], in0=N.qT[:, pair * 4:(pair + 1) * 4, qcol],
                            in1=Knew[:, pair, T + b:T + b + 1].to_broadcast([128, 4]), op=ALU.mult),
                            reads=[N.kT_d[4], N.qT_d[2]], writes=[prd])
                    if not os.environ.get("NSA_NOSN"):
                        for hf in range(2):
                            p.op("pe", lambda e, hf=hf: e.matmul(Sn[0:1, hf * 8:(hf + 1) * 8], lhsT=ind[:, hf:hf + 1], rhs=pr[:, :],
                                                                 start=True, stop=True), reads=[prd, sd], writes=[Snd])
                    SUB = int(os.environ.get("NSA_SUB", "99"))
                    if SUB < 2:
                        continue
                    Es, Esd = self.rot("sEs", 2, [128, 256], BF16, st)
                    En, End = self.rot("sEn", 2, [1, 16], F32, st)
                    if bi == 0:
                        p.op("dve", lambda e: e.tensor_tensor(out=bS[:, :], in0=S[:, 0:256], in1=bS[:, :], op=ALU.add),
                             reads=[Sd, bSd], writes=[bSd])
                        p.op("act", lambda e: e.activation(out=Es[:, 0:256], in_=bS[:, :], func=AF.Exp, scale=SCALE), reads=[bSd], writes=[Esd])
                    else:
                        p.op("act", lambda e, nb_=nb_: e.activation(out=Es[:, 0:nb_ * 16], in_=S[:, 0:nb_ * 16], func=AF.Exp, scale=SCALE),
                             reads=[Sd], writes=[Esd])
                    p.op("act", lambda e: e.activation(out=En[:, :], in_=Sn[0:1, 0:16], func=AF.Exp, scale=SCALE), reads=[Snd], writes=[End])
                    if bi == 1:
                        p.op("dve", lambda e: e.memset(Es[0:1, 0:16], 0.0), reads=[Esd], writes=[Esd])
                    As, Asd = Ac[:, :, 0:65], Acd
                    if SUB < 3:
                        continue
                    for gi in range(4):
                        ab, abd = self.bank()
                        for r in range(4):
                            h = 4 * gi + r
                            for pg in range(nb_):
                                p.op("pe", lambda e, ab=ab, r=r, h=h, gi=gi, pg=pg: e.matmul(
                                    ab[0:1, r * 65:(r + 1) * 65], lhsT=Es[:, pg * 16 + h:pg * 16 + h + 1], rhs=Vx[:, pg, gi, :],
                                    start=(pg == 0), stop=(pg == nb_ - 1)), reads=[Esd, vd], writes=[abd])
                                if pg % 4 == 3:
                                    p.drain("pe")
                        p.op("act", lambda e, ab=ab, gi=gi: e.copy(out=As[:, 4 * gi:4 * gi + 4, :],
                                                                   in_=ab[0:1, 0:260].rearrange("p (a b) -> p a b", a=4)),
                             reads=[abd], writes=[Asd])
                        if SUB < 4:
                            continue
                        tn, tnd = self.rot("stn", 1, [1, 4, 65], F32, st)
                        p.op("dve", lambda e, gi=gi, tn=tn: e.tensor_tensor(
                            out=tn[:, :, :], in0=vn[0:1, bi, gi, :].unsqueeze(1).to_broadcast([1, 4, 65]),
                            in1=En[0:1, (gi % 2) * 8 + (gi // 2) * 4:(gi % 2) * 8 + (gi // 2) * 4 + 4].unsqueeze(2).to_broadcast([1, 4, 65]), op=ALU.mult),
                            reads=[End, vnd], writes=[tnd])
                        p.op("dve", lambda e, gi=gi, tn=tn: e.tensor_tensor(out=As[:, 4 * gi:4 * gi + 4, :], in0=As[:, 4 * gi:4 * gi + 4, :],
                                                                            in1=tn[:, :, :], op=ALU.add), reads=[tnd, Asd], writes=[Asd])
                    if SUB < 5:
                        continue
                    combine(As, Asd, 65, gofs, False)
                if STOP < 9:
                    continue
                for c in range(8):
                    p.op("pe", lambda e, c=c, b=b: e.matmul(ocol[:, c * 16 + b:c * 16 + b + 1],
                                                            lhsT=orow[0:1, 2 * c:2 * c + 2, :].rearrange("p a b -> p (a b)"), rhs=one1[0:1, 0:1],
                                                            start=True, stop=True), reads=[ord_, sd], writes=[ocd])
                p.drain("pe")
            p.op("act", lambda e: e.copy(out=self.xb[:, :, XO:XO + NS], in_=ocol[:, 0:128].rearrange("p (c b) -> p c b", c=8)),
                 reads=[ocd], writes=[self.xb_d[2]])
            self.release(oci)

    def gmlp(self, l):
        p = self.p
        w_in = self.gmlp_w_in
        with ExitStack() as st:
            uT = p.sb([128, 8, 1040], BF16, st=st, name="uT")
            gT = p.sb([128, 8, 1040], BF16, st=st, name="gT")
            wvt = p.sb([128, 8, D], BF16, st=st, name="wvt")
            wvt_d = Dep()
            uT_d = [Dep() for _ in range(3)]
            gT_d = [Dep() for _ in range(3)]
            gc = Dep()
            b_u = p.sb([128, 8], F32, st=st)
            b_v = p.sb([128, D], F32, st=st)
            lg = p.sb([128, D], F32, st=st)
            lb = p.sb([128, D], F32, st=st)
            bsp = p.sb([128, 8, 128], F32, st=st)
            bsp0 = p.sb([128, 8], F32, st=st)
            bsps = p.sb([128, 8, 16], F32, st=st)
            w00 = p.sb([16, 8], F32, st=st)
            WTs = p.sb([16, 8, 16], BF16, st=st)
            WT = p.sb([128, 8, 128], BF16, st=st)
            st0 = ExitStack()
            wsp = p.sb([128, 8, 128], F32, st=st0)
            WTf = p.sb([128, 8, 128], F32, st=st0)
            p.dma(lambda e: e.dma_start(out=b_u[:], in_=self.gmlp_b_in[0:D].rearrange("(c p) -> p c", p=128),
                                        allow_slow_non_contiguous=True), writes=[gc])
            p.dma(lambda e: e.dma_start(out=b_v[:], in_=self.gmlp_b_in[D:2 * D].partition_broadcast(128)), writes=[gc])
            p.dma(lambda e: e.dma_start(out=lg[:], in_=self.gmlp_ln_v[0].partition_broadcast(128)), writes=[gc])
            p.dma(lambda e: e.dma_start(out=lb[:], in_=self.gmlp_ln_v[1].partition_broadcast(128)), writes=[gc])
            p.dma(lambda e: e.dma_start(out=bsp[:].rearrange("p h t -> p (h t)"),
                                        in_=self.gmlp_b_sp.rearrange("h t -> (h t)").partition_broadcast(128)), writes=[gc])
            p.dma(lambda e: e.dma_start(out=bsp0[:], in_=self.gmlp_b_sp[:, 0].partition_broadcast(128),
                                        allow_slow_non_contiguous=True), writes=[gc])
            p.dma(lambda e: e.dma_start(out=w00[:], in_=self.gmlp_w_sp[:, 0, 0].partition_broadcast(16),
                                        allow_slow_non_contiguous=True), writes=[gc])
            p.dma(lambda e: e.dma_start(out=wsp[:], in_=self.gmlp_w_sp.rearrange("h t s -> t h s")), writes=[gc])
            p.op("dve", lambda e: e.tensor_copy(out=bsps[:], in_=bsp0[:].unsqueeze(2).to_broadcast([128, 8, 16])),
                 reads=[gc], writes=[gc])
            p.op("dve", lambda e: e.tensor_tensor(out=WTs[:], in0=self.ident_f[0:16, 0:16].unsqueeze(1).to_broadcast([16, 8, 16]),
                                                  in1=w00[:].unsqueeze(2).to_broadcast([16, 8, 16]), op=ALU.mult),
                 reads=[gc, self.cd], writes=[gc])
            for hh in range(2):
                bk, bd = self.bank()
                for h4 in range(4):
                    h = hh * 4 + h4
                    p.op("pe", lambda e, bk=bk, h=h, h4=h4: e.transpose(out=bk[:, h4 * 128:(h4 + 1) * 128], in_=wsp[:, h, :],
                                                                        identity=self.ident_f[:, :]),
                         reads=[gc, self.cd], writes=[bd])
                p.op("act", lambda e, bk=bk, hh=hh: e.copy(out=WTf[:, hh * 4:hh * 4 + 4, :],
                                                           in_=bk[:, :].rearrange("p (h t) -> p h t", h=4)),
                     reads=[bd], writes=[gc])
            for h in range(8):
                p.op("pool", lambda e, h=h: e.affine_select(out=WT[:, h, :], in_=WTf[:, h, :], pattern=[[1, 128]],
                                                            compare_op=ALU.is_ge, fill=0.0, base=0, channel_multiplier=-1),
                     reads=[gc], writes=[gc])
            p.fence()
            st0.close()
            for g in range(2):
                self.cast_xb(g)
                for c0 in (0, 4):
                    wt, wd = self.getw()
                    wv = wt[:, 0:4096].rearrange("p (k c) -> p k c", k=8)
                    p.dma(lambda e, wv=wv, c0=c0: e.dma_start(
                        out=wv, in_=w_in[:, c0 * 128:(c0 + 4) * 128].rearrange("(k p) c -> p k c", p=128)),
                        writes=[wd], eng="pool")
                    for cc in range(4):
                        c = c0 + cc
                        for tl, ti in enumerate(GROUPS[g]):
                            t0, n = TILES[ti]
                            off = t0 - GSTART[g]
                            bk, bd = self.bank()
                            for k in range(8):
                                p.op("pe", lambda e, bk=bk, wv=wv, k=k, cc=cc, off=off, n=n: e.matmul(
                                    bk[:, :n], lhsT=wv[:, k, cc * 128:(cc + 1) * 128], rhs=self.xb[:, k, off:off + n],
                                    start=(k == 0), stop=(k == 7)), reads=[wd, self.xb_d[tl]], writes=[bd])
                            p.op("act", lambda e, bk=bk, c=c, off=off, n=n: e.activation(
                                out=uT[:, c, off:off + n], in_=bk[:, :n], func=AF.Gelu, bias=b_u[:, c:c + 1], scale=1.0),
                                reads=[bd, gc], writes=[uT_d[tl]])
                wd = wvt_d
                wvv = wvt[:, :, :]
                if g == 0:
                    p.dma(lambda e, wvv=wvv: e.dma_start(out=wvv, in_=w_in[:, D:2 * D].rearrange("(k p) c -> p k c", p=128)),
                          writes=[wd], eng="pool")
                for tl, ti in enumerate(GROUPS[g]):
                    t0, n = TILES[ti]
                    for b0 in range(0, n, 128):
                        nb = min(128, n - b0)
                        off = t0 - GSTART[g] + b0
                        samp = ti == 4
                        vt, vd = self.rot("vt", 1, [128, D], F32, st)
                        for half in range(2):
                            bk, bd = self.bank()
                            for k in range(8):
                                p.op("pe", lambda e, bk=bk, k=k, half=half, off=off, nb=nb: e.matmul(
                                    bk[:nb, :], lhsT=self.xb[:, k, off:off + nb], rhs=wvv[:, k, half * 512:(half + 1) * 512],
                                    start=(k == 0), stop=(k == 7)), reads=[wd, self.xb_d[tl]], writes=[bd])
                            p.op("dve", lambda e, bk=bk, vt=vt, half=half, nb=nb: e.tensor_tensor(
                                out=vt[:nb, half * 512:(half + 1) * 512], in0=bk[:nb, :], in1=b_v[:nb, half * 512:(half + 1) * 512],
                                op=ALU.add), reads=[bd, gc], writes=[vd])
                        sm, smd = self.rot("gsm", 2, [128, 4], F32, st)
                        junk, jd = self.rot("gjunk", 1, [128, D], BF16, st)
                        p.op("act", lambda e, vt=vt, nb=nb: e.activation(out=vt[:nb, :], in_=vt[:nb, :], func=AF.Gelu),
                             reads=[vd], writes=[vd])
                        p.op("dve", lambda e, vt=vt, sm=sm, nb=nb: e.tensor_reduce(out=sm[:nb, 0:1], in_=vt[:nb, :], axis=AX.X, op=ALU.add),
                             reads=[vd], writes=[smd])
                        p.op("dve", lambda e, sm=sm, nb=nb: e.tensor_scalar(out=sm[:nb, 1:2], in0=sm[:nb, 0:1], scalar1=-1.0 / D,
                                                                            scalar2=None, op0=ALU.mult), reads=[smd], writes=[smd])
                        p.op("act", lambda e, vt=vt, sm=sm, nb=nb: e.activation(out=vt[:nb, :], in_=vt[:nb, :], func=AF.Identity,
                                                                                bias=sm[:nb, 1:2], scale=1.0),
                             reads=[vd, smd], writes=[vd])
                        p.op("pool", lambda e, vt=vt, junk=junk, nb=nb: e.tensor_tensor(out=junk[:nb, :], in0=vt[:nb, :], in1=vt[:nb, :],
                                                                                       op=ALU.mult), reads=[vd], writes=[jd])
                        p.op("dve", lambda e, junk=junk, sm=sm, nb=nb: e.tensor_reduce(out=sm[:nb, 2:3], in_=junk[:nb, :], axis=AX.X,
                                                                                      op=ALU.add), reads=[jd, smd], writes=[smd])
                        p.op("act", lambda e, sm=sm, nb=nb: e.activation(out=sm[:nb, 3:4], in_=sm[:nb, 2:3], func=AF.Sqrt,
                                                                         bias=EPS, scale=1.0 / D), reads=[smd], writes=[smd])
                        p.op("dve", lambda e, sm=sm, nb=nb: e.reciprocal(out=sm[:nb, 3:4], in_=sm[:nb, 3:4]), reads=[smd], writes=[smd])
                        p.op("dve", lambda e, vt=vt, sm=sm, nb=nb: e.scalar_tensor_tensor(
                            out=vt[:nb, :], in0=vt[:nb, :], scalar=sm[:nb, 3:4], in1=lg[:nb, :], op0=ALU.mult, op1=ALU.mult),
                            reads=[vd, smd, gc], writes=[vd])
                        p.op("dve", lambda e, vt=vt, nb=nb: e.tensor_tensor(out=vt[:nb, :], in0=vt[:nb, :], in1=lb[:nb, :], op=ALU.add),
                             reads=[vd, gc], writes=[vd])
                        if samp:
                            p.dma(lambda e, vt=vt: e.dma_start(out=self.o_gv_s[:, :], in_=vt[:NS, :]), reads=[vd])
                        vb, vbd = self.rot("vb", 2, [128, D], BF16, st)
                        p.op("act", lambda e, vt=vt, vb=vb, nb=nb: e.copy(out=vb[:nb, :], in_=vt[:nb, :]), reads=[vd], writes=[vbd])
                        for hh in range(2):
                            bk, bd = self.bank()
                            for h4 in range(4):
                                h = hh * 4 + h4
                                rhs = WTs[:NS, h, :] if samp else WT[:, h, :]
                                p.op("pe", lambda e, bk=bk, vb=vb, h=h, h4=h4, nb=nb, rhs=rhs: e.matmul(
                                    bk[:, h4 * 128:h4 * 128 + nb], lhsT=vb[:nb, h * 128:(h + 1) * 128], rhs=rhs,
                                    start=True, stop=True), reads=[vbd, gc], writes=[bd])
                            mt, mtd = self.rot("gmt", 2, [128, 4, 128], F32, st)
                            bias = bsps[:, hh * 4:hh * 4 + 4, :] if samp else bsp[:, hh * 4:hh * 4 + 4, :]
                            p.op("dve", lambda e, bk=bk, mt=mt, nb=nb, bias=bias: e.tensor_tensor(
                                out=mt[:, :, :nb], in0=bk[:, :].rearrange("p (h t) -> p h t", h=4)[:, :, :nb], in1=bias,
                                op=ALU.add), reads=[bd, gc], writes=[mtd])
                            p.op("pool", lambda e, mt=mt, hh=hh, off=off, nb=nb: e.tensor_tensor(
                                out=gT[:, hh * 4:hh * 4 + 4, off:off + nb], in0=mt[:, :, :nb],
                                in1=uT[:, hh * 4:hh * 4 + 4, off:off + nb], op=ALU.mult),
                                reads=[mtd, uT_d[tl]], writes=[gT_d[tl]])
                self.outproj_ln(g, self.gmlp_w_out, 8,
                                lambda k, tl, ti, g=g: gT[:, k, TILES[ti][0] - GSTART[g]:TILES[ti][0] - GSTART[g] + TILES[ti][1]],
                                lambda tl, ti: [gT_d[tl]], l * 2, st)
            p.fence()

    def hgrn(self, l):
        p = self.p
        w_in = self.hgrn_w_in
        with ExitStack() as st:
            hc = Dep()
            gT = p.sb([128, 8, 1040], BF16, st=st, name="hgT")
            gT_d = [Dep() for _ in range(3)]
            S = p.sb([128, 8, 128], F32, st=st, name="S")
            Sb = p.sb([128, 8, 128], BF16, st=st, name="Sb")
            S_d = [Dep() for _ in range(8)]
            lg4 = p.sb([128, 8, 4], F32, st=st)
            lbv = p.sb([128, 8], F32, st=st)
            oml = p.sb([128, 8], F32, st=st)
            ssum = p.sb([128, 8], F32, st=st)
            onec = p.sb([128, 1], F32, st=st)
            gain = p.sb([128, 1], F32, st=st)
            mask = p.sb([128, 128], F32, st=st)
            p.op("dve", lambda e: e.memset(S[:], 0.0), writes=S_d)
            p.op("dve", lambda e: e.memset(Sb[:], 0.0), writes=S_d)
            p.op("dve", lambda e: e.memset(onec[:], 1.0), writes=[hc])
            p.op("dve", lambda e: e.memset(mask[:], 1.0), writes=[hc])
            p.op("pool", lambda e: e.affine_select(out=mask[:], in_=mask[:], pattern=[[1, 128]], compare_op=ALU.is_ge,
                                                   fill=0.0, base=0, channel_multiplier=-1), reads=[hc], writes=[hc])
            p.op("dve", lambda e: e.memset(mask[0:64, 64:128], 0.0), reads=[hc], writes=[hc])
            p.dma(lambda e: e.dma_start(out=gain[:], in_=self.hgrn_norm_gain.rearrange("(p o) -> p o", o=1)), writes=[hc])
            for l4 in range(4):
                p.dma(lambda e, l4=l4: e.dma_start(out=lg4[:, :, l4], in_=self.hgrn_lb_logits[l4].rearrange("(h p) -> p h", p=128),
                                                   allow_slow_non_contiguous=True), writes=[hc])
            p.op("act", lambda e: e.activation(out=lg4[:], in_=lg4[:], func=AF.Exp), reads=[hc], writes=[hc])
            p.op("dve", lambda e: e.tensor_reduce(out=ssum[:], in_=lg4[:], axis=AX.X, op=ALU.add), reads=[hc], writes=[hc])
            p.op("dve", lambda e: e.tensor_reduce(out=lbv[:], in_=lg4[:, :, 1:l + 1], axis=AX.X, op=ALU.add), reads=[hc], writes=[hc])
            p.op("dve", lambda e: e.reciprocal(out=ssum[:], in_=ssum[:]), reads=[hc], writes=[hc])
            p.op("dve", lambda e: e.tensor_tensor(out=lbv[:], in0=lbv[:], in1=ssum[:], op=ALU.mult), reads=[hc], writes=[hc])
            p.op("dve", lambda e: e.tensor_scalar(out=oml[:], in0=lbv[:], scalar1=-1.0, scalar2=1.0, op0=ALU.mult, op1=ALU.add),
                 reads=[hc], writes=[hc])

            def T2(name, dt=F32, n=1, w=512):
                return self.rot(name, n, [128, w], dt, st)

            for g in range(2):
                self.cast_xb(g)
                for h in range(8):
                    wt, wd = self.getw()
                    wv = wt[:, 0:4096].rearrange("p (k s c) -> p k s c", k=8, s=4)
                    for s4 in range(4):
                        p.dma(lambda e, wv=wv, s4=s4, h=h: e.dma_start(
                            out=wv[:, :, s4, :], in_=w_in[:, s4 * D + h * 128:s4 * D + (h + 1) * 128].rearrange("(k p) c -> p k c", p=128)),
                            writes=[wd], eng="pool")
                    for tl, ti in enumerate(GROUPS[g]):
                        t0, n = TILES[ti]
                        off = t0 - GSTART[g]
                        samp = ti == 4
                        bks = []
                        for s4 in range(4):
                            bk, bd = self.bank()
                            for k in range(8):
                                p.op("pe", lambda e, bk=bk, wv=wv, k=k, s4=s4, off=off, n=n: e.matmul(
                                    bk[:, :n], lhsT=wv[:, k, s4, :], rhs=self.xb[:, k, off:off + n], start=(k == 0), stop=(k == 7)),
                                    reads=[wd, self.xb_d[tl]], writes=[bd])
                            bks.append((bk, bd))
                        qf, qd = T2("hq")
                        fg, fd = T2("hf")
                        kk, kd = T2("hk")
                        it, idp = T2("hi")
                        sg, sd = T2("hs")
                        oT, od = T2("ho")
                        p.op("act", lambda e, qf=qf, n=n, b=bks[0][0]: e.copy(out=qf[:, :n], in_=b[:, :n]), reads=[bks[0][1]], writes=[qd])
                        p.op("act", lambda e, fg=fg, n=n, b=bks[1][0]: e.activation(out=fg[:, :n], in_=b[:, :n], func=AF.Sigmoid),
                             reads=[bks[1][1]], writes=[fd])
                        p.op("dve", lambda e, fg=fg, n=n, h=h: e.tensor_scalar(out=fg[:, :n], in0=fg[:, :n], scalar1=oml[:, h:h + 1],
                                                                               scalar2=lbv[:, h:h + 1], op0=ALU.mult, op1=ALU.add),
                             reads=[fd, hc], writes=[fd])
                        p.op("dve", lambda e, fg=fg, kk=kk, n=n: e.tensor_scalar(out=kk[:, :n], in0=fg[:, :n], scalar1=-1.0, scalar2=1.0,
                                                                                 op0=ALU.mult, op1=ALU.add), reads=[fd], writes=[kd])
                        p.op("act", lambda e, it=it, n=n, b=bks[2][0]: e.copy(out=it[:, :n], in_=b[:, :n]), reads=[bks[2][1]], writes=[idp])
                        p.op("act", lambda e, sg=sg, n=n, b=bks[3][0]: e.activation(out=sg[:, :n], in_=b[:, :n], func=AF.Silu),
                             reads=[bks[3][1]], writes=[sd])
                        if not samp:
                            lf, ld = T2("hl")
                            G, Gd = T2("hG")
                            qt, qtd = T2("hqt", BF16)
                            kt, ktd = T2("hkt", BF16)
                            p.op("act", lambda e, lf=lf, fg=fg: e.activation(out=lf[:, :], in_=fg[:, :], func=AF.Ln), reads=[fd], writes=[ld])
                            p.op("dve", lambda e, G=G, lf=lf: e.tensor_tensor_scan(
                                out=G[:, :], data0=onec[:, 0:1].to_broadcast([128, 512]), data1=lf[:, :], initial=0.0,
                                op0=ALU.mult, op1=ALU.add), reads=[ld, hc], writes=[Gd])
                            G3 = G[:, :].rearrange("p (c t) -> p c t", t=64)
                            l3 = lf[:, :].rearrange("p (c t) -> p c t", t=64)
                            p.op("dve", lambda e, G3=G3, l3=l3: e.tensor_tensor(
                                out=l3[:, 1:8, :], in0=G3[:, 1:8, :], in1=G3[:, 0:7, 63:64].to_broadcast([128, 7, 64]), op=ALU.subtract),
                                reads=[Gd, ld], writes=[ld])
                            p.op("dve", lambda e, G3=G3, l3=l3: e.tensor_copy(out=l3[:, 0, :], in_=G3[:, 0, :]), reads=[Gd, ld], writes=[ld])
                            p.op("act", lambda e, G=G, lf=lf: e.activation(out=G[:, :], in_=lf[:, :], func=AF.Exp, scale=-1.0),
                                 reads=[ld, Gd], writes=[Gd])
                            p.op("act", lambda e, lf=lf: e.activation(out=lf[:, :], in_=lf[:, :], func=AF.Exp), reads=[ld, Gd], writes=[ld])
                            p.op("dve", lambda e, qt=qt, qf=qf, lf=lf: e.tensor_tensor(out=qt[:, :], in0=qf[:, :], in1=lf[:, :], op=ALU.mult),
                                 reads=[qd, ld], writes=[qtd])
                            p.op("dve", lambda e, kk=kk, G=G: e.tensor_tensor(out=kk[:, :], in0=kk[:, :], in1=G[:, :], op=ALU.mult),
                                 reads=[kd, Gd], writes=[kd])
                            p.op("act", lambda e, kt=kt, kk=kk: e.copy(out=kt[:, :], in_=kk[:, :]), reads=[kd], writes=[ktd])
                            for b in range(4):
                                cs = slice(b * 128, (b + 1) * 128)
                                ab, abd = self.bank()
                                p.op("pe", lambda e, ab=ab, kt=kt, qt=qt, cs=cs: e.matmul(ab[:, 0:128], lhsT=kt[:, cs], rhs=qt[:, cs],
                                                                                          start=True, stop=True),
                                     reads=[ktd, qtd], writes=[abd])
                                am, amd = self.rot("ham", 2, [128, 128], BF16, st)
                                p.op("dve", lambda e, am=am, ab=ab: e.tensor_tensor(out=am[:, :], in0=ab[:, 0:128], in1=mask[:, :], op=ALU.mult),
                                     reads=[abd, hc], writes=[amd])
                                tb, tbd = self.bank()
                                p.op("pe", lambda e, tb=tb, it=it, cs=cs: e.transpose(out=tb[:, 0:128], in_=it[:, cs], identity=self.ident_f[:, :]),
                                     reads=[idp, self.cd], writes=[tbd])
                                p.op("pe", lambda e, tb=tb, kk=kk, cs=cs: e.transpose(out=tb[:, 128:256], in_=kk[:, cs], identity=self.ident_f[:, :]),
                                     reads=[kd, self.cd], writes=[tbd])
                                tk, tkd = self.rot("htk", 2, [128, 2, 128], BF16, st)
                                p.op("act", lambda e, tk=tk, tb=tb: e.copy(out=tk[:, :, :], in_=tb[:, 0:256].rearrange("p (a b) -> p a b", a=2)),
                                     reads=[tbd], writes=[tkd])
                                ob, obd = self.bank()
                                p.op("pe", lambda e, ob=ob, tk=tk, am=am: e.matmul(ob[:, 0:128], lhsT=tk[:, 0, :], rhs=am[:, :],
                                                                                   start=True, stop=False), reads=[tkd, amd], writes=[obd])
                                for ch in range(2):
                                    ps_ = slice(ch * 64, (ch + 1) * 64)
                                    cc = slice(b * 128 + ch * 64, b * 128 + (ch + 1) * 64)
                                    p.op("pe", lambda e, ob=ob, qt=qt, ch=ch, cc=cc, h=h: e.matmul(
                                        ob[:, ch * 64:(ch + 1) * 64], lhsT=Sb[:, h, :], rhs=qt[:, cc], start=False, stop=(ch == 1)),
                                        reads=[S_d[h], qtd], writes=[obd])
                                    su, sud = self.bank()
                                    p.op("pe", lambda e, su=su, tk=tk, ps_=ps_: e.matmul(su[:, 0:128], lhsT=tk[ps_, 1, :], rhs=tk[ps_, 0, :],
                                                                                         start=True, stop=True), reads=[tkd], writes=[sud])
                                    p.op("dve", lambda e, su=su, h=h: e.tensor_tensor(out=S[:, h, :], in0=S[:, h, :], in1=su[:, 0:128], op=ALU.add),
                                         reads=[sud, S_d[h]], writes=[S_d[h]])
                                    dcol = b * 128 + ch * 64 + 63
                                    p.op("dve", lambda e, lf=lf, h=h, dcol=dcol: e.tensor_scalar(
                                        out=S[:, h, :], in0=S[:, h, :], scalar1=lf[:, dcol:dcol + 1], scalar2=None, op0=ALU.mult),
                                        reads=[S_d[h], ld], writes=[S_d[h]])
                                    p.op("act", lambda e, h=h: e.copy(out=Sb[:, h, :], in_=S[:, h, :]), reads=[S_d[h]], writes=[S_d[h]])
                                p.op("act", lambda e, oT=oT, ob=ob, cs=cs: e.copy(out=oT[:, cs], in_=ob[:, 0:128]), reads=[obd], writes=[od])
                            if ti == 3:
                                p.dma(lambda e, h=h: e.dma_start(out=self.o_hg_p[h], in_=S[:, h, :]), reads=[S_d[h]])
                        else:
                            tb, tbd = self.bank()
                            p.op("pe", lambda e, tb=tb, it=it: e.transpose(out=tb[:NS, 0:128], in_=it[:, 0:NS], identity=self.ident_f[:, :]),
                                 reads=[idp, self.cd], writes=[tbd])
                            p.op("pe", lambda e, tb=tb, kk=kk: e.transpose(out=tb[:NS, 128:256], in_=kk[:, 0:NS], identity=self.ident_f[:, :]),
                                 reads=[kd, self.cd], writes=[tbd])
                            tk, tkd = self.rot("hstk", 1, [NS, 2, 128], F32, st)
                            p.op("act", lambda e, tk=tk, tb=tb: e.copy(out=tk[:, :, :], in_=tb[:NS, 0:256].rearrange("p (a b) -> p a b", a=2)),
                                 reads=[tbd], writes=[tkd])
                            ktb, ktbd = self.rot("hsk", 1, [NS, 128], BF16, st)
                            Z, Zd = self.rot("hsZ", 1, [NS, NS, 128], BF16, st)
                            p.op("act", lambda e, ktb=ktb, tk=tk: e.copy(out=ktb[:, :], in_=tk[:, 1, :]), reads=[tkd], writes=[ktbd])
                            p.op("dve", lambda e, Z=Z, tk=tk: e.tensor_tensor(
                                out=Z[:, :, :], in0=tk[:, 0, :].unsqueeze(1).to_broadcast([NS, NS, 128]),
                                in1=self.ident_f[0:NS, 0:NS].unsqueeze(2).to_broadcast([NS, NS, 128]), op=ALU.mult),
                                reads=[tkd, self.cd], writes=[Zd])
                            ob, obd = self.bank()
                            for b4 in range(4):
                                kb, kbd = self.bank()
                                p.op("pe", lambda e, kb=kb, ktb=ktb, Z=Z, b4=b4: e.matmul(
                                    kb[:, :], lhsT=ktb[:, :], rhs=Z[:, b4 * 4:(b4 + 1) * 4, :].rearrange("p a b -> p (a b)"),
                                    start=True, stop=True), reads=[ktbd, Zd], writes=[kbd])
                                for bb in range(4):
                                    b = b4 * 4 + bb
                                    s0, s0d = self.rot("hs0", 2, [128, 128], F32, st)
                                    p.dma(lambda e, s0=s0, b=b, h=h: e.dma_start(out=s0[:, :], in_=self.state_hgrn[b, h]), writes=[s0d])
                                    p.op("dve", lambda e, s0=s0, fg=fg, kb=kb, b=b, bb=bb: e.scalar_tensor_tensor(
                                        out=s0[:, :], in0=s0[:, :], scalar=fg[:, b:b + 1], in1=kb[:, bb * 128:(bb + 1) * 128],
                                        op0=ALU.mult, op1=ALU.add), reads=[s0d, fd, kbd], writes=[s0d])
                                    p.dma(lambda e, s0=s0, b=b, h=h: e.dma_start(out=self.o_hg_s[b, h], in_=s0[:, :]), reads=[s0d])
                                    p.op("pe", lambda e, ob=ob, s0=s0, qf=qf, b=b: e.matmul(ob[:, b:b + 1], lhsT=s0[:, :], rhs=qf[:, b:b + 1],
                                                                                            start=True, stop=True), reads=[s0d, qd], writes=[obd])
                            p.op("act", lambda e, oT=oT, ob=ob: e.copy(out=oT[:, 0:NS], in_=ob[:, 0:NS]), reads=[obd], writes=[od])
                        osq, osd = T2("hosq", BF16)
                        p.op("pool", lambda e, osq=osq, oT=oT, n=n: e.tensor_tensor(out=osq[:, :n], in0=oT[:, :n], in1=oT[:, :n], op=ALU.mult),
                             reads=[od], writes=[osd])
                        rb, rbd = self.bank()
                        p.op("pe", lambda e, rb=rb, osq=osq, n=n: e.matmul(rb[:, :n], lhsT=self.ones_bf[:, :], rhs=osq[:, :n], start=True, stop=True),
                             reads=[osd, self.cd], writes=[rbd])
                        rr, rrd = T2("hrr")
                        p.op("act", lambda e, rr=rr, rb=rb, n=n: e.activation(out=rr[:, :n], in_=rb[:, :n], func=AF.Sqrt, bias=EPS, scale=1.0 / 128),
                             reads=[rbd], writes=[rrd])
                        p.op("dve", lambda e, rr=rr, n=n: e.reciprocal(out=rr[:, :n], in_=rr[:, :n]), reads=[rrd], writes=[rrd])
                        p.op("dve", lambda e, rr=rr, oT=oT, n=n: e.tensor_tensor(out=rr[:, :n], in0=rr[:, :n], in1=oT[:, :n], op=ALU.mult),
                             reads=[rrd, od], writes=[rrd])
                        p.op("dve", lambda e, rr=rr, sg=sg, h=h, off=off, n=n: e.scalar_tensor_tensor(
                            out=gT[:, h, off:off + n], in0=rr[:, :n], scalar=gain[:, 0:1], in1=sg[:, :n], op0=ALU.mult, op1=ALU.mult),
                            reads=[rrd, sd, hc], writes=[gT_d[tl]])
                self.outproj_ln(g, self.hgrn_w_out, 8,
                                lambda k, tl, ti, g=g: gT[:, k, TILES[ti][0] - GSTART[g]:TILES[ti][0] - GSTART[g] + TILES[ti][1]],
                                lambda tl, ti: [gT_d[tl]], l * 2, st)
            p.fence()


_CACHE = {}


def _program(n_phys):
    if n_phys not in _CACHE:
        k = K(n_phys=n_phys)
        _CACHE[n_phys] = k.p.finish()
    return _CACHE[n_phys]


def kernel(x_prompt, x_sample, cache_cmp_kv, cache_slc_kv, cache_win_kv, state_hgrn, page_table,
           ln_gain, ln_bias, ffn_w_in, ffn_w_out, nsa_w_in, nsa_b_gate, nsa_w_cmp, nsa_pe_cmp, nsa_w_out,
           gmlp_w_in, gmlp_b_in, gmlp_ln_v, gmlp_w_sp, gmlp_b_sp, gmlp_w_out,
           hgrn_w_in, hgrn_lb_logits, hgrn_norm_gain, hgrn_w_out):
    f = lambda a: np.ascontiguousarray(np.asarray(a, dtype=np.float32))
    n_phys = cache_cmp_kv.shape[1]
    nc = _program(n_phys)
    ccmp = f(cache_cmp_kv).reshape(2 * n_phys * 128, 512)
    cslc = f(cache_slc_kv).reshape(2 * n_phys * 128, 512)
    shared = dict(
        cache_cmp_kv=ccmp, cache_slc_kv=cslc, ln_gain=f(ln_gain), ln_bias=f(ln_bias), ffn_w_in=f(ffn_w_in),
        ffn_w_out=f(ffn_w_out), nsa_w_in=f(nsa_w_in), nsa_b_gate=f(nsa_b_gate), nsa_w_cmp=f(nsa_w_cmp),
        nsa_pe_cmp=f(nsa_pe_cmp), nsa_w_out=f(nsa_w_out), gmlp_w_in=f(gmlp_w_in)[0], gmlp_b_in=f(gmlp_b_in)[0],
        gmlp_ln_v=f(gmlp_ln_v)[0], gmlp_w_sp=f(gmlp_w_sp)[0], gmlp_b_sp=f(gmlp_b_sp)[0], gmlp_w_out=f(gmlp_w_out)[0],
        hgrn_w_in=f(hgrn_w_in)[0], hgrn_lb_logits=f(hgrn_lb_logits), hgrn_norm_gain=f(hgrn_norm_gain)[0],
        hgrn_w_out=f(hgrn_w_out)[0])
    xp, xs = f(x_prompt), f(x_sample)
    cw = f(cache_win_kv).reshape(2, 128, 512, 512)
    sh = f(state_hgrn)
    pt = np.ascontiguousarray(np.asarray(page_table, dtype=np.int32))
    in_maps = []
    for c in range(8):
        sl = slice(NS * c, NS * (c + 1))
        m = dict(shared)
        m.update(x_prompt=xp[c], x_sample=np.ascontiguousarray(xs[sl, 0]), cache_win_kv=np.ascontiguousarray(cw[:, sl]),
                 state_hgrn=np.ascontiguousarray(sh[0, sl]), page_table=np.ascontiguousarray(pt[sl]))
        in_maps.append(m)
    res = run_bass_kernel_spmd(nc, in_maps, core_ids=list(range(8))).results
    cat = lambda name, ax: np.concatenate([r[name] for r in res], axis=ax)
    y_prompt = np.stack([r["y_prompt"] for r in res])
    y_sample = cat("y_sample", 0).reshape(128, 1, D)
    kvp = lambda name: np.concatenate([r[name].reshape(2, 16, 128, 2, 4, 64) for r in res], axis=1)
    kvs = lambda name: np.concatenate([r[name].reshape(2, NS, 1, 2, 4, 64) for r in res], axis=1)
    win_p = np.stack([r["o_win_p"].reshape(2, 512, 2, 4, 64) for r in res], axis=1)
    gv = cat("o_gv_s", 0).reshape(1, 128, 1, D)
    hg_p = np.stack([r["o_hg_p"] for r in res])[None]
    hg_s = cat("o_hg_s", 0)[None]
    return (y_prompt, y_sample, kvp("o_cmp_p"), kvs("o_cmp_s"), kvp("o_slc_p"), kvs("o_slc_s"),
            win_p, kvs("o_win_s"), gv, hg_p, hg_s)
```
